# Optimizing a Trainium2 kernel written in Bass

```python
import jax, jax.numpy as jnp
from jax import lax
import numpy as np

D_MODEL = 1024
BATCH = 16
SEQ = 2048
DEPTH = 2
DEC_BATCH = 8
DEC_SEQ = 4096
PAST_LEN = 128

N_HEADS = 8
N_KV_HEADS = 2
HEAD_DIM = 128
GROUP = N_HEADS // N_KV_HEADS
ATTN_W = N_HEADS * HEAD_DIM
KV_W = N_KV_HEADS * HEAD_DIM
Q_BLOCK = 128
ROPE_THETA = 10000.0
ROPE_AXIS_DIM = HEAD_DIM // 2
GRID_W = 64
D_CONV = 1024
CONV_K = 31
N_BRANCH = 2
D_FF = -(-8 * D_MODEL // (3 * 256)) * 256
EPS = 1e-6
D_IN = ATTN_W + 2 * KV_W + 2 * D_CONV + N_BRANCH * D_MODEL

kernel_name = "hybrid_conv_axial_gqa_encoder"


def rms_norm(x, g):
    xf = x.astype(jnp.float32)
    y = xf * lax.rsqrt(jnp.mean(xf * xf, axis=-1, keepdims=True) + EPS)
    return (y * g.astype(jnp.float32)).astype(x.dtype)


def layer_norm(x, g, b):
    xf = x.astype(jnp.float32)
    mu = jnp.mean(xf, axis=-1, keepdims=True)
    xc = xf - mu
    var = jnp.mean(xc * xc, axis=-1, keepdims=True)
    y = xc * lax.rsqrt(var + EPS)
    return (y * g.astype(jnp.float32) + b.astype(jnp.float32)).astype(x.dtype)


def axial_rope_tables(seq_len, dtype):
    rows = seq_len // GRID_W
    row = jnp.repeat(jnp.arange(rows), GRID_W).astype(jnp.float32)
    col = jnp.tile(jnp.arange(GRID_W), rows).astype(jnp.float32)
    half = ROPE_AXIS_DIM // 2
    inv = ROPE_THETA ** (-jnp.arange(half, dtype=jnp.float32) / half)
    ang = jnp.stack([row[:, None] * inv, col[:, None] * inv], axis=1)
    ang = ang[:, None]
    return jnp.cos(ang).astype(dtype), jnp.sin(ang).astype(dtype)


def apply_axial_rope(x, cos, sin):
    B, S, H, D = x.shape
    xr = x.reshape(B, S, H, 2, 2, ROPE_AXIS_DIM // 2)
    x1, x2 = xr[..., 0, :], xr[..., 1, :]
    o1 = x1 * cos - x2 * sin
    o2 = x2 * cos + x1 * sin
    return jnp.stack([o1, o2], axis=-2).reshape(B, S, H, D)


def blocked_attention(q, k, v):
    B, S = q.shape[0], q.shape[1]
    nblk = S // Q_BLOCK
    qb = q.reshape(B, nblk, Q_BLOCK, N_KV_HEADS, GROUP, HEAD_DIM).transpose(1, 0, 2, 3, 4, 5)
    scale = HEAD_DIM ** -0.5

    def one_block(qblk):
        s = jnp.einsum('bqkgd,bskd->bkgqs', qblk, k).astype(jnp.float32) * scale
        p = jax.nn.softmax(s, axis=-1).astype(v.dtype)
        return jnp.einsum('bkgqs,bskd->bqkgd', p, v)

    o = lax.map(one_block, qb)
    return o.transpose(1, 0, 2, 3, 4, 5).reshape(B, S, ATTN_W)


def conv_module(u, w_dw, b_dw, ln_g, ln_b, w_pw):
    a, g = jnp.split(u, 2, axis=-1)
    z = a * jax.nn.sigmoid(g)
    z = lax.conv_general_dilated(
        z, w_dw[:, None, :].astype(z.dtype), window_strides=(1,),
        padding=[(CONV_K // 2, CONV_K // 2)],
        dimension_numbers=('NWC', 'WIO', 'NWC'),
        feature_group_count=D_CONV) + b_dw
    z = jax.nn.silu(layer_norm(z, ln_g, ln_b))
    return z @ w_pw


def trunk(x, g_mix, w_in, g_q, g_k, w_dw, b_dw, ln_g, ln_b, w_pw, b_gate,
          w_out, g_ffn, w_gu, w_down, g_final):
    B, S, _ = x.shape
    cos, sin = axial_rope_tables(S, x.dtype)
    splits = [ATTN_W, ATTN_W + KV_W, ATTN_W + 2 * KV_W, ATTN_W + 2 * KV_W + 2 * D_CONV]
    for l in range(DEPTH):
        h = rms_norm(x, g_mix[l])
        proj = h @ w_in[l]
        q, k, v, u, gl = jnp.split(proj, splits, axis=-1)
        q = rms_norm(q.reshape(B, S, N_HEADS, HEAD_DIM), g_q[l])
        k = rms_norm(k.reshape(B, S, N_KV_HEADS, HEAD_DIM), g_k[l])
        q = apply_axial_rope(q, cos, sin)
        k = apply_axial_rope(k, cos, sin)
        v = v.reshape(B, S, N_KV_HEADS, HEAD_DIM)
        a = blocked_attention(q, k, v)
        c = conv_module(u, w_dw[l], b_dw[l], ln_g[l], ln_b[l], w_pw[l])
        gates = jax.nn.sigmoid(gl + b_gate[l])
        ga, gc = jnp.split(gates, 2, axis=-1)
        x = x + (ga * a + gc * c) @ w_out[l]
        h2 = rms_norm(x, g_ffn[l])
        gate, up = jnp.split(h2 @ w_gu[l], 2, axis=-1)
        x = x + (jax.nn.silu(gate) * up) @ w_down[l]
    return rms_norm(x, g_final)


def setup_inputs(seed: int = 0) -> dict:
    key = jax.random.key(seed)
    ks = jax.random.split(key, 20)
    f32 = jnp.float32

    def nrm(k, shape, scale):
        return jax.random.normal(k, shape, f32) * scale

    def gain(k, shape):
        return 1.0 + 0.02 * jax.random.normal(k, shape, f32)

    return {
        "x_prompt": nrm(ks[0], (BATCH, SEQ, D_MODEL), 1.0),
        "x_sample": nrm(ks[1], (DEC_BATCH, DEC_SEQ, D_MODEL), 1.0),
        "g_mix": gain(ks[2], (DEPTH, D_MODEL)),
        "w_in": nrm(ks[3], (DEPTH, D_MODEL, D_IN), D_MODEL ** -0.5),
        "g_q": gain(ks[4], (DEPTH, HEAD_DIM)),
        "g_k": gain(ks[5], (DEPTH, HEAD_DIM)),
        "w_dw": nrm(ks[6], (DEPTH, CONV_K, D_CONV), CONV_K ** -0.5),
        "b_dw": nrm(ks[7], (DEPTH, D_CONV), 0.02),
        "ln_g": gain(ks[8], (DEPTH, D_CONV)),
        "ln_b": nrm(ks[9], (DEPTH, D_CONV), 0.02),
        "w_pw": nrm(ks[10], (DEPTH, D_CONV, D_MODEL), D_CONV ** -0.5),
        "b_gate": nrm(ks[11], (DEPTH, N_BRANCH * D_MODEL), 0.02),
        "w_out": nrm(ks[12], (DEPTH, D_MODEL, D_MODEL), D_MODEL ** -0.5),
        "g_ffn": gain(ks[13], (DEPTH, D_MODEL)),
        "w_gu": nrm(ks[14], (DEPTH, D_MODEL, 2 * D_FF), D_MODEL ** -0.5),
        "w_down": nrm(ks[15], (DEPTH, D_FF, D_MODEL), D_FF ** -0.5),
        "g_final": gain(ks[16], (D_MODEL,)),
    }


def reference(x_prompt, x_sample, g_mix, w_in, g_q, g_k, w_dw, b_dw, ln_g, ln_b,
              w_pw, b_gate, w_out, g_ffn, w_gu, w_down, g_final):
    y_prompt = trunk(x_prompt, g_mix, w_in, g_q, g_k, w_dw, b_dw, ln_g, ln_b, w_pw,
                     b_gate, w_out, g_ffn, w_gu, w_down, g_final)
    y_sample = trunk(x_sample, g_mix, w_in, g_q, g_k, w_dw, b_dw, ln_g, ln_b, w_pw,
                     b_gate, w_out, g_ffn, w_gu, w_down, g_final)
    return (y_prompt, y_sample)
```

```python
import numpy as np
from contextlib import ExitStack
import concourse.bass as bass
import concourse.mybir as mybir
from concourse.bass_utils import run_bass_kernel_spmd

F32 = mybir.dt.float32
BF16 = mybir.dt.bfloat16
AF = mybir.ActivationFunctionType
ALU = mybir.AluOpType

ENGS = ("pe", "act", "dve", "pool", "sp")

D = 1024
NH = 8
NKV = 2
HD = 128
DFF = 2816
NFF = DFF // 128
CK = 31
EPS = 1e-6
L = 2
T = 512
SMAX = 4096
GRID_W = 64
ROPE_THETA = 10000.0

WA = 20 * 1024
WB = (8 + 16 + 8 + 8 + 44) * 1024 + 8 * DFF
WTOT = WA + WB
LOADE = 4096
NS = 3
NDVE_CONV = 8
CONV_A = 4
CONV_Q = 1
CONV_FFN_GU = 4
CONV_FFN_DN = 5
CONV_PROJ = 3
CONV_ATT_FRAC = 0.92

PL = 306
C_GMIX, C_GFFN, C_LNG, C_LNB, C_BDW, C_BG, C_WDW, C_GQ, C_GK = 0, 8, 16, 24, 32, 40, 56, 304, 305
C_GF = 2 * PL
NPAR = C_GF + 8


class Prog:
    def __init__(self, nc, stack, n_dma_sems=24):
        self.nc = nc
        self.q = {e: [] for e in ENGS}
        self.sem = {e: stack.enter_context(nc.semaphore("c_" + e)) for e in ENGS}
        self.cnt = {e: 0 for e in ENGS}
        self.seen = {e: {f: 0 for f in ENGS} for e in ENGS}
        self.seen_dma = {e: {} for e in ENGS}
        self.snap = {}
        self.regions = {}
        self.dma_pool = {}
        self.dma_sems = []
        for qn in ("sp", "act"):
            n = n_dma_sems if qn == "sp" else 16
            self.dma_pool[qn] = []
            for i in range(n):
                h = stack.enter_context(nc.semaphore(f"d_{qn}{i}"))
                self.dma_pool[qn].append([h, 0, len(self.dma_sems)])
                self.dma_sems.append(h)
        self.dma_rr = {qn: 0 for qn in self.dma_pool}
        self.nwaits = 0
        self.nops = {e: 0 for e in ENGS}

    def _deps(self, eng, reads, writes, is_dma):
        deps = []
        for (r, lo, hi) in reads:
            for a in self.regions.get(r, ()):
                if a[2] and a[0] < hi and lo < a[1]:
                    deps.append(a[3])
        for (r, lo, hi) in writes:
            for a in self.regions.get(r, ()):
                if a[0] < hi and lo < a[1]:
                    ev = a[3]
                    if (not is_dma) and ev[0] == "E" and ev[1] == eng:
                        continue
                    deps.append(ev)
        return deps

    def _record(self, ev, reads, writes):
        for (r, lo, hi) in writes:
            lst = self.regions.setdefault(r, [])
            lst[:] = [a for a in lst if not (lo <= a[0] and a[1] <= hi)]
            lst.append((lo, hi, True, ev))
        for (r, lo, hi) in reads:
            lst = self.regions.setdefault(r, [])
            if ev[0] == "E":
                ke = ev[1]
                lst[:] = [a for a in lst if not ((not a[2]) and a[0] == lo and a[1] == hi
                                                 and a[3][0] == "E" and a[3][1] == ke)]
            lst.append((lo, hi, False, ev))

    def _inherit(self, eng, ev):
        s = self.snap.get(ev)
        if s is None:
            return
        seen = self.seen[eng]
        for f, v in s[0].items():
            if f != eng and v > seen[f]:
                seen[f] = v
        sd = self.seen_dma[eng]
        for d, v in s[1].items():
            if v > sd.get(d, 0):
                sd[d] = v

    def _emit_waits(self, eng, deps):
        waits = []
        seen = self.seen[eng]
        sdma = self.seen_dma[eng]
        need_e, need_d = {}, {}
        for ev in deps:
            if ev[0] == "E":
                if ev[2] > need_e.get(ev[1], 0):
                    need_e[ev[1]] = ev[2]
            else:
                if ev[2] > need_d.get(ev[1], 0):
                    need_d[ev[1]] = ev[2]
        for f, v in need_e.items():
            if seen[f] >= v:
                continue
            waits.append((self.sem[f], v))
            self._inherit(eng, ("E", f, v))
            seen[f] = max(seen[f], v)
        for d, v in need_d.items():
            if sdma.get(d, 0) >= v:
                continue
            waits.append((self.dma_sems[d], v))
            sdma[d] = v
            self._inherit(eng, ("D", d, v))
        self.nwaits += len(waits)
        return waits

    def op(self, eng, fns, reads=(), writes=()):
        if callable(fns):
            fns = [fns]
        deps = self._deps(eng, reads, writes, False)
        waits = self._emit_waits(eng, deps)
        self.cnt[eng] += 1
        ev = ("E", eng, self.cnt[eng])
        self.snap[ev] = (dict(self.seen[eng]), dict(self.seen_dma[eng]))
        sem = self.sem[eng]

        def run(e, waits=waits, fns=fns, sem=sem):
            for (s, val) in waits:
                e.wait_ge(s, val)
            for f in fns[:-1]:
                f(e)
            fns[-1](e).then_inc(sem, 1)
        self.q[eng].append(run)
        self._record(ev, reads, writes)
        self.nops[eng] += len(fns)
        return ev

    def dma(self, qn, out, in_, reads=(), writes=()):
        deps = self._deps(qn, reads, writes, True)
        pool = self.dma_pool[qn]
        i = self.dma_rr[qn]
        self.dma_rr[qn] = (i + 1) % len(pool)
        slot = pool[i]
        d = slot[2]
        if slot[1] > 0:
            deps.append(("D", d, slot[1]))
        waits = self._emit_waits(qn, deps)
        slot[1] += 16
        ev = ("D", d, slot[1])
        self.snap[ev] = (dict(self.seen[qn]), dict(self.seen_dma[qn]))
        sem = slot[0]

        def run(e, waits=waits, out=out, in_=in_, sem=sem):
            for (s, val) in waits:
                e.wait_ge(s, val)
            e.dma_start(out=out, in_=in_).then_inc(sem, 16)
        self.q[qn].append(run)
        self._record(ev, reads, writes)
        self.nops[qn] += 1
        return ev

    def barrier_all(self):
        evs = [("E", f, self.cnt[f]) for f in ENGS if self.cnt[f] > 0]
        for qn, pool in self.dma_pool.items():
            for slot in pool:
                if slot[1] > 0:
                    evs.append(("D", slot[2], slot[1]))
        for e in ENGS:
            waits = self._emit_waits(e, list(evs))
            if waits:
                def run(eo, waits=waits):
                    for (s, val) in waits:
                        eo.wait_ge(s, val)
                self.q[e].append(run)
        self.regions = {}

    def run_blocks(self):
        with self.nc.Block() as block:
            @block.tensor
            def _(e):
                for f in self.q["pe"]:
                    f(e)

            @block.scalar
            def _(e):
                for f in self.q["act"]:
                    f(e)

            @block.vector
            def _(e):
                for f in self.q["dve"]:
                    f(e)

            @block.gpsimd
            def _(e):
                for f in self.q["pool"]:
                    f(e)

            @block.sync
            def _(e):
                for f in self.q["sp"]:
                    f(e)


class SB:
    def __init__(self, sb_ap):
        self.sb = sb_ap
        self.off = 0

    def alloc(self, nbytes):
        o = self.off
        self.off += (nbytes + 63) // 64 * 64
        return o

    def view(self, off, nbytes, dt, c=None):
        v = self.sb[:, off // 4:(off + nbytes) // 4]
        if dt != F32:
            v = v.bitcast(dt)
        if c is not None:
            v = v.rearrange("p (c t) -> p c t", c=c)
        return v


class T3:
    _n = [0]

    def __init__(self, sbm, off, C, W, dt, region=None, rbase=None):
        self.ds = 4 if dt == F32 else 2
        self.C, self.W = C, W
        self.ap = sbm.view(off, C * W * self.ds, dt, c=C)
        if region is None:
            T3._n[0] += 1
            region, rbase = "t%d" % T3._n[0], off
        self.region = region
        self.off = off - rbase

    def ch(self, c, a=0, b=None):
        b = self.W if b is None else b
        lo = self.off + (c * self.W + a) * self.ds
        hi = self.off + (c * self.W + b) * self.ds
        return self.ap[:, c, a:b], (self.region, lo, hi)

    def iv(self, c0=0, c1=None):
        c1 = self.C if c1 is None else c1
        return (self.region, self.off + c0 * self.W * self.ds, self.off + c1 * self.W * self.ds)


def build_program(seqs):
    NT = sum(seqs) // T
    nc = bass.Bass("TRN2", target_bir_lowering=False)
    xT = nc.dram_tensor("xT", [NT, 128, 8, T], F32, kind="ExternalInput").ap()
    wall = nc.dram_tensor("wall", [L, 128, WTOT], F32, kind="ExternalInput").ap()
    par_d = nc.dram_tensor("par", [128, NPAR], F32, kind="ExternalInput").ap()
    cst_d = nc.dram_tensor("cst", [2, 128, SMAX], F32, kind="ExternalInput").ap()
    swp_d = nc.dram_tensor("swp", [128, 128], F32, kind="ExternalInput").ap()
    yT = nc.dram_tensor("yT", [NT, 128, 8, T], F32, kind="ExternalOutput").ap()
    wsc = nc.dram_tensor("wsc", [L, 128, WTOT], BF16, kind="Internal").ap()
    xs = nc.dram_tensor("xs", [NT, 128, 8, T], F32, kind="Internal").ap()
    zs = nc.dram_tensor("zs", [NT, 128, 8, T], BF16, kind="Internal").ap()

    with ExitStack() as st:
        SBBYTES = 212480
        sb_t = st.enter_context(nc.sbuf_tensor("sb", [128, SBBYTES // 4], F32))
        ps_t = st.enter_context(nc.psum_tensor("ps", [128, 8, 512], F32))
        P = Prog(nc, st)
        M = SB(sb_t)

        def PS(b, n=512):
            return ps_t[:, b, 0:n], ("ps", b * 2048, b * 2048 + n * 4)

        o_par = M.alloc(NPAR * 4)
        par = M.view(o_par, NPAR * 4, F32)
        o_swb = M.alloc(256)
        swb = M.view(o_swb, 256, BF16)
        o_one = M.alloc(256)
        ones = M.view(o_one, 256, BF16)
        o_eps = M.alloc(64)
        epsc = M.view(o_eps, 4, F32)
        KT = T3(M, M.alloc(NKV * SMAX * 2), NKV, SMAX, BF16)
        Vt_off = M.alloc((SMAX // 128) * 256 * 2)
        Vt = T3(M, Vt_off, SMAX // 128, 256, BF16)
        o_tile = M.off
        xtb = [T3(M, M.alloc(8 * T * 4), 8, T, F32), T3(M, M.alloc(8 * T * 4), 8, T, F32)]
        XT = {"cur": xtb[0]}
        sq = T3(M, M.alloc(8 * T * 2), 8, T, BF16)
        hTb = [T3(M, M.alloc(8 * T * 2), 8, T, BF16), T3(M, M.alloc(8 * T * 2), 8, T, BF16)]
        HT = {"cur": hTb[0]}

        class _CurH:
            def ch(self, *a, **k):
                return HT["cur"].ch(*a, **k)

            def iv(self, *a, **k):
                return HT["cur"].iv(*a, **k)
        hT = _CurH()
        QT_off = M.alloc(8 * T * 2)
        QT = T3(M, QT_off, 8, T, BF16, "qg", QT_off)
        gat_off = M.alloc(16 * T * 2)
        assert gat_off == QT_off + 8 * T * 2
        gat = T3(M, gat_off, 16, T, BF16, "qg", QT_off)
        hh = T3(M, QT_off, NFF, T, BF16, "qg", QT_off)
        o_big = M.alloc(8 * T * 4 + 8 * (T + 30) * 2)
        acc = T3(M, o_big, 8, T, F32, "big", o_big)
        zb = T3(M, o_big + 8 * T * 4, 8, T + 30, BF16, "big", o_big)
        mc = T3(M, M.alloc(8 * T * 4), 8, T, F32)
        mm = T3(M, QT_off, 8, T, BF16, "qg", QT_off)
        lnt = T3(M, M.alloc(2 * T * 4), 2, T, F32)
        PT = T3(M, M.alloc(4 * T * 2), 4, T, BF16)
        qgb = T3(M, M.alloc(2 * T * 2), 2, T, BF16)
        cs = T3(M, M.alloc(2 * T * 4), 2, T, F32)
        NMISC = 5
        rsn = T3(M, M.alloc(2 * T * 4), 2, T, F32)
        misc = T3(M, M.alloc(NMISC * T * 4), NMISC, T, F32)
        ring_off = M.alloc(NS * LOADE * 2)
        ring = T3(M, ring_off, NS, LOADE, BF16)
        assert M.off <= SBBYTES, M.off

        def pcol(c):
            return par[:, c:c + 1]

        P.dma("sp", par, par_d, writes=[("sb", o_par, o_par + NPAR * 4)])
        swf_t = T3(M, ring_off, 1, 128, F32, "swf", ring_off)
        swf = swf_t.ch(0)
        P.dma("sp", swf[0], swp_d, writes=[swf[1]])
        P.op("dve", lambda e: e.tensor_copy(out=swb, in_=swf[0]), reads=[swf[1]],
             writes=[("sb", o_swb, o_swb + 256)])
        P.op("pool", lambda e: e.memset(ones, 1.0), writes=[("sb", o_one, o_one + 256)])
        P.op("pool", lambda e: e.memset(epsc, EPS), writes=[("sb", o_eps, o_eps + 4)])

        P.barrier_all()

        assert seqs[0] <= SMAX // 2
        CVP = 1024
        stg = []
        for k_ in range(2):
            o_ = (SMAX // 128) * 256 * 2 // 2 + k_ * CVP * 4
            stg.append((M.view(Vt_off + o_, CVP * 4, F32), (Vt.region, o_, o_ + CVP * 4)))
        cvt = {"i": 0, "done": set()}

        sched = []
        for S in seqs:
            for l in range(L):
                for _ in range(S // T):
                    sched.append((l, "A"))
                for _ in range(S // T):
                    sched.append((l, "B"))
        loads = []
        for (l, ph) in sched:
            if ph == "A":
                for k in range(5):
                    loads.append((l, k * LOADE, LOADE))
            else:
                base = WA
                for k in range(21):
                    loads.append((l, base + k * LOADE, LOADE))
                base += 21 * LOADE
                for k in range(8):
                    loads.append((l, base + k * DFF, DFF))
        wst = {"cur": -1, "issued": 0}

        def w_acquire():
            wst["cur"] += 1
            cur = wst["cur"]
            while wst["issued"] < min(len(loads), cur + NS):
                i = wst["issued"]
                l, off, n = loads[i]
                s = i % NS
                ap, iv = ring.ch(s, 0, n)
                wreg = ("wsc%d" % l, off * 2, (off + n) * 2)
                if (l, off) in cvt["done"]:
                    P.dma("sp", ap, wsc[l, :, off:off + n], reads=[wreg], writes=[iv])
                else:
                    cvt["done"].add((l, off))
                    for a_ in range(0, n, CVP):
                        b_ = min(n, a_ + CVP)
                        sg = stg[cvt["i"] % 2]
                        sgap = sg[0][:, 0:b_ - a_]
                        sgiv = (sg[1][0], sg[1][1], sg[1][1] + (b_ - a_) * 4)
                        P.dma("sp", sgap, wall[l, :, off + a_:off + b_], writes=[sgiv])
                        dst = ring.ch(s, a_, b_)
                        if False:
                            P.op("pool", lambda e, dst=dst, sgap=sgap: e.tensor_copy(out=dst[0], in_=sgap),
                                 reads=[sgiv], writes=[dst[1]])
                        else:
                            P.op("act", lambda e, dst=dst, sgap=sgap: e.activation(out=dst[0], in_=sgap, func=AF.Copy),
                                 reads=[sgiv], writes=[dst[1]])
                        cvt["i"] += 1
                    P.dma("sp", wsc[l, :, off:off + n], ap, reads=[iv], writes=[wreg])
                wst["issued"] += 1
            return cur % NS

        def w_tile(s, u, kc=8, width=128, off=None):
            o = u * 1024 if off is None else off
            ap, iv = ring.ch(s, o, o + kc * width)
            return ap.rearrange("p (k m) -> p k m", k=kc), iv

        gen_banks = {"i": 0}

        def gbank():
            b = gen_banks["i"] % 8
            gen_banks["i"] += 1
            return b

        misc_i = {"i": 0}

        def mtile():
            i = misc_i["i"] % NMISC
            misc_i["i"] += 1
            return misc.ch(i)

        def mm_group(out, pairs, reads):
            n = len(pairs)
            fns = [(lambda e, a=a, b=b, i=i: e.matmul(out[0], lhsT=a, rhs=b, start=(i == 0), stop=(i == n - 1)))
                   for i, (a, b) in enumerate(pairs)]
            P.op("pe", fns, reads=reads, writes=[out[1]])

        def rstd_from(ps_sum, nfeat):
            va = mtile()
            P.op("act", lambda e: e.activation(out=va[0], in_=ps_sum[0], func=AF.Ln, scale=1.0 / nfeat, bias=epsc),
                 reads=[ps_sum[1]], writes=[va[1]])
            rs = mtile()
            P.op("act", lambda e: e.activation(out=rs[0], in_=va[0], func=AF.Exp, scale=-0.5),
                 reads=[va[1]], writes=[rs[1]])
            return rs

        SQ_ENG = ("act", "pool", "act", "dve", "act", "pool", "act", "dve")

        def norm1a(xb):
            for c in range(8):
                a, o = xb.ch(c), sq.ch(c)
                if SQ_ENG[c] == "act":
                    P.op("act", lambda e, a=a, o=o: e.activation(out=o[0], in_=a[0], func=AF.Square),
                         reads=[a[1]], writes=[o[1]])
                else:
                    P.op(SQ_ENG[c], lambda e, a=a, o=o: e.tensor_tensor(out=o[0], in0=a[0], in1=a[0], op=ALU.mult),
                         reads=[a[1]], writes=[o[1]])

        def norm1b(rs):
            pb = PS(gbank())
            mm_group(pb, [(ones, sq.ch(c)[0]) for c in range(8)], reads=[sq.iv()])
            va = mtile()
            P.op("act", lambda e: e.activation(out=va[0], in_=pb[0], func=AF.Ln, scale=1.0 / D, bias=epsc),
                 reads=[pb[1]], writes=[va[1]])
            P.op("act", lambda e: e.activation(out=rs[0], in_=va[0], func=AF.Exp, scale=-0.5),
                 reads=[va[1]], writes=[rs[1]])

        def norm1(xb, rs):
            norm1a(xb)
            norm1b(rs)

        def norm2(xb, rs, gcol0, hbuf=None):
            hbuf = HT["cur"] if hbuf is None else hbuf
            for c in range(8):
                a, o = xb.ch(c), hbuf.ch(c)
                P.op("dve", lambda e, a=a, o=o, c=c: e.scalar_tensor_tensor(
                    out=o[0], in0=a[0], scalar=pcol(gcol0 + c), in1=rs[0], op0=ALU.mult, op1=ALU.mult),
                    reads=[a[1], rs[1]], writes=[o[1]])

        def rmsnorm_to_hT(gcol0):
            rs = mtile()
            norm1(XT["cur"], rs)
            norm2(XT["cur"], rs, gcol0)

        pend = {"rs": None, "h": False}

        def start_norm(gcol0):
            if pend["rs"] is None:
                rs = rsn.ch(0)
                norm1(XT["cur"], rs)
            else:
                rs = pend["rs"]
            if not pend["h"]:
                norm2(XT["cur"], rs, gcol0)
            pend["rs"], pend["h"] = None, False

        def qk_s1(pb, gcol, slot):
            qg = qgb.ch(slot)
            s2 = sq.ch(slot)
            P.op("act", lambda e: e.activation(out=qg[0], in_=pb[0], func=AF.Copy, scale=pcol(gcol)),
                 reads=[pb[1]], writes=[qg[1]])
            P.op("act", lambda e: e.activation(out=s2[0], in_=pb[0], func=AF.Square),
                 reads=[pb[1]], writes=[s2[1]])

        def qk_s2(out, slot):
            qg = qgb.ch(slot)
            s2 = sq.ch(slot)
            pss = PS(gbank())
            mm_group(pss, [(ones, s2[0])], reads=[s2[1]])
            psw = PS(gbank())
            mm_group(psw, [(swb, qg[0])], reads=[qg[1]])
            rs = rstd_from(pss, HD)
            t1, t2 = mtile(), mtile()
            cosv, sinv = cs.ch(0), cs.ch(1)
            P.op("pool", lambda e: e.tensor_tensor(out=t1[0], in0=qg[0], in1=cosv[0], op=ALU.mult),
                 reads=[qg[1], cosv[1]], writes=[t1[1]])
            P.op("dve", lambda e: e.tensor_tensor(out=t2[0], in0=psw[0], in1=sinv[0], op=ALU.mult),
                 reads=[psw[1], sinv[1]], writes=[t2[1]])
            P.op("pool", lambda e: e.tensor_tensor(out=t1[0], in0=t1[0], in1=t2[0], op=ALU.add),
                 reads=[t1[1], t2[1]], writes=[t1[1]])
            P.op("pool", lambda e: e.tensor_tensor(out=out[0], in0=t1[0], in1=rs[0], op=ALU.mult),
                 reads=[t1[1], rs[1]], writes=[out[1]])

        XQ = {"q": "act"}
        IDX = {"i": 0}
        DEFER = []

        def flush_deferred():
            while DEFER:
                DEFER.pop(0)()

        def load_x(l, g, xb):
            src = xT if l == 0 else xs
            rd = [] if l == 0 else [("xs", g, g + 1)]
            P.dma(XQ["q"], xb.ap, src[g], reads=rd, writes=[xb.iv()])

        def load_cs(tpos):
            for i in range(2):
                ap, iv = cs.ch(i)
                P.dma("act", ap, cst_d[i, :, tpos:tpos + T], writes=[iv])

        def phase_a(l, g, ti, nt, nxt, xnext):
            pc = l * PL
            xt = XT["cur"]
            load_cs(ti * T)
            start_norm(pc + C_GMIX)
            s = w_acquire()
            wv, wviv = w_tile(s, 2, kc=8, width=256)

            def vblk(blk):
                pb = PS(gbank(), 256)
                mm_group(pb, [(hT.ch(k, blk * 128, (blk + 1) * 128)[0], wv[:, k, :]) for k in range(8)],
                         reads=[wviv, hT.iv()])
                o = Vt.ch(ti * 4 + blk)
                P.op("act", lambda e, o=o, pb=pb: e.activation(out=o[0], in_=pb[0], func=AF.Copy),
                     reads=[pb[1]], writes=[o[1]])
            for j in range(NKV):
                w, wiv = w_tile(s, j)
                pb = PS(gbank())
                mm_group(pb, [(w[:, k, :], hT.ch(k)[0]) for k in range(8)], reads=[wiv, hT.iv()])
                qk_s1(pb, pc + C_GK, j)
            flush_deferred()
            if nxt is not None:
                load_x(nxt[0], nxt[3], xnext)
            g_b0 = g - ti
            pre = ti >= 2
            if pre and BG["for"] != (l, g_b0):
                load_zb(g_b0, 0, nt)
                build_bg(l, g_b0, hTb[(IDX["i"] + nt - ti) % 2])
            vblk(0)
            vblk(1)
            qk_s2(KT.ch(0, ti * T, (ti + 1) * T), 0)
            vblk(2)
            qk_s2(KT.ch(1, ti * T, (ti + 1) * T), 1)
            if nxt is not None:
                pend["rs"] = rsn.ch(1 - pend.get("k", 0))
                pend["k"] = 1 - pend.get("k", 0)
                norm1a(xnext)
            vblk(3)
            for c in range(8):
                if nxt is not None and c == 1:
                    norm1b(pend["rs"])
                if nxt is not None and c == 2:
                    norm2(xnext, pend["rs"], nxt[0] * PL + C_GMIX, HT["next"])
                    pend["h"] = True
                if c % 2 == 0:
                    s = w_acquire()
                wa, waiv = w_tile(s, (c % 2) * 2)
                wg, wgiv = w_tile(s, (c % 2) * 2 + 1)
                pa, pg = PS(gbank()), PS(gbank())
                mm_group(pa, [(wa[:, k, :], hT.ch(k)[0]) for k in range(8)], reads=[waiv, hT.iv()])
                mm_group(pg, [(wg[:, k, :], hT.ch(k)[0]) for k in range(8)], reads=[wgiv, hT.iv()])
                sg = mtile()
                P.op("act", lambda e, sg=sg, pg=pg: e.activation(out=sg[0], in_=pg[0], func=AF.Sigmoid),
                     reads=[pg[1]], writes=[sg[1]])
                o = QT.ch(c)
                P.op("dve", lambda e, o=o, pa=pa, sg=sg: e.tensor_tensor(out=o[0], in0=pa[0], in1=sg[0], op=ALU.mult),
                     reads=[pa[1], sg[1]], writes=[o[1]])
                if pre:
                    conv_step(CONV_A, True)
            DEFER.append(lambda: P.dma("act", zs[g], QT.ap, reads=[QT.iv()], writes=[("zs", g, g + 1)]))

        BG = {"q": [], "i": 0, "ntap": 0, "for": None}

        def load_zb(g, ti, nt):
            P.dma("act", zb.ap[:, :, 15:15 + T], zs[g], reads=[("zs", g, g + 1)], writes=[zb.iv()])
            if ti > 0:
                P.dma("act", zb.ap[:, :, 0:15], zs[g - 1, :, :, T - 15:T], reads=[("zs", g - 1, g)], writes=[zb.iv()])
            else:
                P.op("pool", lambda e: e.memset(zb.ap[:, :, 0:15], 0.0), writes=[zb.iv()])
            if ti < nt - 1:
                P.dma("act", zb.ap[:, :, 15 + T:30 + T], zs[g + 1, :, :, 0:15], reads=[("zs", g + 1, g + 2)],
                      writes=[zb.iv()])
            else:
                P.op("pool", lambda e: e.memset(zb.ap[:, :, 15 + T:30 + T], 0.0), writes=[zb.iv()])

        def build_bg(l, g, hT):
            pc = l * PL
            bg = []

            def tap_dve(k, c):
                zi, a = zb.ch(c, k, k + T), acc.ch(c)
                wcol = pcol(pc + C_WDW + k * 8 + c)
                if k == 0:
                    return lambda: P.op("dve", lambda e: e.tensor_scalar(
                        out=a[0], in0=zi[0], scalar1=wcol, scalar2=pcol(pc + C_BDW + c),
                        op0=ALU.mult, op1=ALU.add), reads=[zi[1]], writes=[a[1]])
                return lambda: P.op("dve", lambda e: e.scalar_tensor_tensor(
                    out=a[0], in0=zi[0], scalar=wcol, in1=a[0], op0=ALU.mult, op1=ALU.add),
                    reads=[zi[1], a[1]], writes=[a[1]])

            for k in range(CK):
                for c in range(8):
                    bg.append(tap_dve(k, c))
            ntap = len(bg)
            mu, lrs = lnt.ch(0), lnt.ch(1)

            def ln_evac(c):
                a, o1, o2 = acc.ch(c), hT.ch(c), sq.ch(c)

                def f():
                    P.op("act", lambda e: e.activation(out=o1[0], in_=a[0], func=AF.Copy), reads=[a[1]], writes=[o1[1]])
                    P.op("act", lambda e: e.activation(out=o2[0], in_=a[0], func=AF.Square), reads=[a[1]], writes=[o2[1]])
                return f
            for c in range(8):
                bg.append(ln_evac(c))
            bg.extend([None] * 12)
            pm = PS(3)
            bg.append(lambda: mm_group(pm, [(ones, hT.ch(c)[0]) for c in range(8)], reads=[hT.iv()]))
            bg.extend([None] * 4)
            bg.append(lambda: P.op("dve", lambda e: e.tensor_scalar(out=mu[0], in0=pm[0], scalar1=1.0 / D, scalar2=None,
                                                                     op0=ALU.mult), reads=[pm[1]], writes=[mu[1]]))
            bg.append(lambda: mm_group(pm, [(ones, sq.ch(c)[0]) for c in range(8)], reads=[sq.iv()]))
            bg.extend([None] * 4)

            def ln_rstd():
                vv = mtile()
                P.op("dve", lambda e: e.tensor_tensor(out=vv[0], in0=mu[0], in1=mu[0], op=ALU.mult),
                     reads=[mu[1]], writes=[vv[1]])
                P.op("dve", lambda e: e.scalar_tensor_tensor(out=vv[0], in0=pm[0], scalar=1.0 / D, in1=vv[0],
                                                             op0=ALU.mult, op1=ALU.subtract),
                     reads=[pm[1], vv[1]], writes=[vv[1]])
                P.op("act", lambda e: e.activation(out=vv[0], in_=vv[0], func=AF.Ln, bias=epsc),
                     reads=[vv[1]], writes=[vv[1]])
                P.op("act", lambda e: e.activation(out=lrs[0], in_=vv[0], func=AF.Exp, scale=-0.5),
                     reads=[vv[1]], writes=[lrs[1]])
            bg.append(ln_rstd)

            def ln_sub(c):
                a = acc.ch(c)
                return lambda: P.op("dve", lambda e: e.tensor_tensor(out=a[0], in0=a[0], in1=mu[0], op=ALU.subtract),
                                    reads=[a[1], mu[1]], writes=[a[1]])

            def ln_mul(c):
                a = acc.ch(c)
                return lambda: P.op("dve", lambda e: e.tensor_tensor(out=a[0], in0=a[0], in1=lrs[0], op=ALU.mult),
                                    reads=[a[1], lrs[1]], writes=[a[1]])

            tts = {}

            def ln_aff(c):
                a = acc.ch(c)
                return lambda: P.op("act", lambda e: e.activation(
                    out=a[0], in_=a[0], func=AF.Identity, scale=pcol(pc + C_LNG + c), bias=pcol(pc + C_LNB + c)),
                    reads=[a[1]], writes=[a[1]])

            def ln_tanh(c):
                a = acc.ch(c)

                def f():
                    tts[c] = mtile()
                    tt = tts[c]
                    P.op("act", lambda e: e.activation(out=tt[0], in_=a[0], func=AF.Tanh, scale=0.5),
                         reads=[a[1]], writes=[tt[1]])
                return f

            def ln_out(c):
                a, o = acc.ch(c), hT.ch(c)

                def f():
                    tt = tts[c]
                    P.op("dve", lambda e: e.scalar_tensor_tensor(out=o[0], in0=tt[0], scalar=1.0, in1=a[0],
                                                                 op0=ALU.add, op1=ALU.mult),
                         reads=[tt[1], a[1]], writes=[o[1]])
                return f
            for c in range(8):
                bg.append(ln_sub(c))
            for c in range(8):
                bg.append(ln_mul(c))
            for c in range(8):
                bg.append(ln_aff(c))
            for c in range(10):
                if c < 8:
                    bg.append(ln_tanh(c))
                if c >= 2:
                    bg.append(ln_out(c - 2))
            BG["q"], BG["i"], BG["ntap"], BG["for"] = bg, 0, ntap, (l, g)

        def conv_step(n, taps_only=False):
            lim = BG["ntap"] if taps_only else len(BG["q"])
            while n > 0 and BG["i"] < lim:
                f = BG["q"][BG["i"]]
                BG["i"] += 1
                n -= 1
                if f is not None:
                    f()

        def phase_b(l, g, ti, nt, nxt, xnext):
            pc = l * PL
            S = nt * T
            nkc = S // 128
            xt = XT["cur"]
            load_cs(ti * T)
            if ti == 0:
                flush_deferred()
            if BG["for"] != (l, g):
                flush_deferred()
                load_zb(g, ti, nt)
                build_bg(l, g, HT["cur"])
            start_norm(pc + C_GMIX)
            conv_ops = BG["q"]
            cst_ = BG
            def gate_grp(c, s):
                w, wiv = w_tile(s, c % 4)
                pb = PS(gbank())
                mm_group(pb, [(w[:, k, :], hT.ch(k)[0]) for k in range(8)], reads=[wiv, hT.iv()])
                o = gat.ch(c)
                P.op("act", lambda e, o=o, pb=pb, c=c: e.activation(out=o[0], in_=pb[0], func=AF.Sigmoid,
                                                                  bias=pcol(pc + C_BG + c)),
                     reads=[pb[1]], writes=[o[1]])

            for h in range(NH):
                if h % 4 == 0:
                    s = w_acquire()
                w, wiv = w_tile(s, h % 4)
                pb = PS(gbank())
                mm_group(pb, [(w[:, k, :], hT.ch(k)[0]) for k in range(8)], reads=[wiv, hT.iv()])
                qk_s1(pb, pc + C_GQ, h % 2)
                if h == 2:
                    flush_deferred()
                conv_step(CONV_Q, True)
                if h >= 1:
                    qk_s2(QT.ch(h - 1), (h - 1) % 2)
            qk_s2(QT.ch(NH - 1), (NH - 1) % 2)
            for c in range(16):
                if c % 4 == 0:
                    s = w_acquire()
                gate_grp(c, s)
                conv_step(CONV_PROJ, True)
            items = [(h, kc) for h in range(NH) for kc in range(nkc)]
            LA = 2
            scale = float(HD) ** -0.5

            def head_epilogue(h):
                par_ = h % 2
                pv, sm = PS(4 + 2 * par_), PS(5 + 2 * par_)
                rc, an = mtile(), mtile()
                P.op("act", lambda e: e.activation(out=rc[0], in_=sm[0], func=AF.Ln), reads=[sm[1]], writes=[rc[1]])
                P.op("act", lambda e: e.activation(out=rc[0], in_=rc[0], func=AF.Exp, scale=-1.0), reads=[rc[1]], writes=[rc[1]])
                P.op("dve", lambda e: e.tensor_tensor(out=an[0], in0=pv[0], in1=rc[0], op=ALU.mult),
                     reads=[pv[1], rc[1]], writes=[an[1]])
                ga, o = gat.ch(h), mc.ch(h)
                P.op("pool", lambda e: e.tensor_tensor(out=o[0], in0=an[0], in1=ga[0], op=ALU.mult),
                     reads=[an[1], ga[1]], writes=[o[1]])

            n_it = len(items)
            for i in range(n_it + LA):
                if i < n_it:
                    h, kc = items[i]
                    j = h // (NH // NKV)
                    pb = PS(i % 3)
                    kt = KT.ch(j, kc * 128, (kc + 1) * 128)
                    q = QT.ch(h)
                    mm_group(pb, [(kt[0], q[0])], reads=[kt[1], q[1]])
                    pt = PT.ch(i % 4)
                    P.op("act", lambda e, pt=pt, pb=pb: e.activation(out=pt[0], in_=pb[0], func=AF.Exp, scale=scale),
                         reads=[pb[1]], writes=[pt[1]])
                    if i == 4 and nxt is not None:
                        load_x(nxt[0], nxt[3], xnext)
                    rem = len(conv_ops) - cst_["i"]
                    if rem > 0:
                        left = max(1, int(n_it * CONV_ATT_FRAC) - i)
                        conv_step(-(-rem // left))
                if i >= LA:
                    ii = i - LA
                    h, kc = items[ii]
                    j = h // (NH // NKV)
                    par_ = h % 2
                    pv, sm = PS(4 + 2 * par_), PS(5 + 2 * par_)
                    pt = PT.ch(ii % 4)
                    va, viv = Vt.ch(kc, j * 128, (j + 1) * 128)
                    first, last = (kc == 0), (kc == nkc - 1)
                    fns = [lambda e, pv=pv, va=va, pt=pt, first=first, last=last:
                           e.matmul(pv[0], lhsT=va, rhs=pt[0], start=first, stop=last),
                           lambda e, sm=sm, pt=pt, first=first, last=last:
                           e.matmul(sm[0], lhsT=ones, rhs=pt[0], start=first, stop=last)]
                    P.op("pe", fns, reads=[viv, pt[1]], writes=[pv[1], sm[1]])
                    if last:
                        head_epilogue(h)
            conv_step(len(conv_ops))
            for c in range(8):
                if c % 4 == 0:
                    s = w_acquire()
                w, wiv = w_tile(s, c % 4)
                pb = PS(gbank())
                mm_group(pb, [(w[:, k, :], hT.ch(k)[0]) for k in range(8)], reads=[wiv, hT.iv()])
                gc, tm, ma, o = gat.ch(8 + c), mtile(), mc.ch(c), mm.ch(c)
                P.op("dve", lambda e, tm=tm, pb=pb, gc=gc: e.scalar_tensor_tensor(
                    out=tm[0], in0=pb[0], scalar=0.5, in1=gc[0], op0=ALU.mult, op1=ALU.mult),
                    reads=[pb[1], gc[1]], writes=[tm[1]])
                P.op("pool", lambda e, tm=tm, ma=ma, o=o: e.tensor_tensor(out=o[0], in0=tm[0], in1=ma[0], op=ALU.add),
                     reads=[tm[1], ma[1]], writes=[o[1]])
            for c in range(8):
                if c % 4 == 0:
                    s = w_acquire()
                w, wiv = w_tile(s, c % 4)
                pb = PS(gbank())
                mm_group(pb, [(w[:, k, :], mm.ch(k)[0]) for k in range(8)], reads=[wiv, mm.iv()])
                x_ = xt.ch(c)
                P.op("dve", lambda e, x_=x_, pb=pb: e.tensor_tensor(out=x_[0], in0=pb[0], in1=x_[0], op=ALU.add),
                     reads=[pb[1], x_[1]], writes=[x_[1]])
                o_ = sq.ch(c)
                P.op("act", lambda e, x_=x_, o_=o_: e.activation(out=o_[0], in_=x_[0], func=AF.Square),
                     reads=[x_[1]], writes=[o_[1]])
            pipe = nxt is not None and nxt[1] == "B" and nxt[0] == l
            if pipe:
                load_zb(nxt[3], nxt[2], nt)
                build_bg(l, nxt[3], HT["next"])
            rs_f = mtile()
            norm1b(rs_f)
            norm2(xt, rs_f, pc + C_GFFN)
            for jf in range(NFF):
                if jf % 2 == 0:
                    s = w_acquire()
                wg, wgiv = w_tile(s, (jf % 2) * 2)
                wu, wuiv = w_tile(s, (jf % 2) * 2 + 1)
                pg, pu = PS(gbank()), PS(gbank())
                mm_group(pg, [(wg[:, k, :], hT.ch(k)[0]) for k in range(8)], reads=[wgiv, hT.iv()])
                mm_group(pu, [(wu[:, k, :], hT.ch(k)[0]) for k in range(8)], reads=[wuiv, hT.iv()])
                sg = mtile()
                P.op("act", lambda e, sg=sg, pg=pg: e.activation(out=sg[0], in_=pg[0], func=AF.Silu),
                     reads=[pg[1]], writes=[sg[1]])
                o = hh.ch(jf)
                P.op("dve", lambda e, o=o, pu=pu, sg=sg: e.tensor_tensor(out=o[0], in0=pu[0], in1=sg[0], op=ALU.mult),
                     reads=[pu[1], sg[1]], writes=[o[1]])
                if pipe:
                    conv_step(CONV_FFN_GU, True)
                if jf == 6 and nxt is not None:
                    norm1a(xnext)
            if nxt is not None:
                pend["rs"] = rsn.ch(1 - pend.get("k", 0))
                pend["k"] = 1 - pend.get("k", 0)
                norm1b(pend["rs"])
                norm2(xnext, pend["rs"], nxt[0] * PL + C_GMIX, HT["next"])
                pend["h"] = True
            for c in range(8):
                s = w_acquire()
                w, wiv = w_tile(s, 0, kc=NFF, width=128, off=0)
                pb = PS(gbank())
                mm_group(pb, [(w[:, k, :], hh.ch(k)[0]) for k in range(NFF)], reads=[wiv, hh.iv()])
                x_ = xt.ch(c)
                P.op("dve", lambda e, x_=x_, pb=pb: e.tensor_tensor(out=x_[0], in0=pb[0], in1=x_[0], op=ALU.add),
                     reads=[pb[1], x_[1]], writes=[x_[1]])
                if pipe:
                    conv_step(CONV_FFN_DN, True)
            if l < L - 1:
                DEFER.append(lambda: P.dma("act", xs[g], xt.ap, reads=[xt.iv()], writes=[("xs", g, g + 1)]))
            else:
                for c in range(8):
                    a, o = xt.ch(c), sq.ch(c)
                    P.op("act", lambda e, a=a, o=o: e.activation(out=o[0], in_=a[0], func=AF.Square),
                         reads=[a[1]], writes=[o[1]])
                pb = PS(gbank())
                mm_group(pb, [(ones, sq.ch(c)[0]) for c in range(8)], reads=[sq.iv()])
                rs = rstd_from(pb, D)
                for c in range(8):
                    a, o = xt.ch(c), mc.ch(c)
                    P.op("dve", lambda e, a=a, o=o, c=c: e.scalar_tensor_tensor(
                        out=o[0], in0=a[0], scalar=pcol(C_GF + c), in1=rs[0], op0=ALU.mult, op1=ALU.mult),
                        reads=[a[1], rs[1]], writes=[o[1]])
                DEFER.append(lambda: P.dma("act", yT[g], mc.ap, reads=[mc.iv()], writes=[("yT", g, g + 1)]))

        entries = []
        g0 = 0
        for S in seqs:
            nt = S // T
            for l in range(L):
                for ti in range(nt):
                    entries.append((l, "A", ti, g0 + ti, nt))
                for ti in range(nt):
                    entries.append((l, "B", ti, g0 + ti, nt))
            g0 += nt
        load_x(entries[0][0], entries[0][3], xtb[0])
        for i, (l, ph, ti, g, nt) in enumerate(entries):
            XT["cur"] = xtb[i % 2]
            IDX["i"] = i
            HT["cur"], HT["next"] = hTb[i % 2], hTb[(i + 1) % 2]
            nxt = entries[i + 1] if i + 1 < len(entries) else None
            if ph == "A":
                phase_a(l, g, ti, nt, nxt, xtb[(i + 1) % 2])
            else:
                phase_b(l, g, ti, nt, nxt, xtb[(i + 1) % 2])
        flush_deferred()
        assert wst["cur"] == len(loads) - 1, (wst, len(loads))
        P.barrier_all()
        P.run_blocks()
        build_program.stats = (dict(P.nops), P.nwaits)
    return nc


def _tile_std(W):
    K, Mo = W.shape
    kc, mj = K // 128, Mo // 128
    return W.reshape(kc, 128, mj, 128).transpose(1, 2, 0, 3).reshape(128, mj, kc * 128)


def prep_weights(w_in, w_pw, w_out, w_gu, w_down):
    out = np.empty((L, 128, WTOT), np.float32)
    for l in range(L):
        wi = w_in[l]
        wq = _tile_std(wi[:, 0:1024])
        wk = _tile_std(wi[:, 1024:1280])
        wv = wi[:, 1280:1536].reshape(8, 128, 256).transpose(1, 0, 2).reshape(128, 2048)
        wa = _tile_std(wi[:, 1536:2560])
        wg = _tile_std(wi[:, 2560:3584])
        wgl = _tile_std(wi[:, 3584:5632])
        glu = np.stack([wa, wg], axis=2).reshape(128, 16 * 1024)
        gg = _tile_std(w_gu[l][:, :DFF])
        uu = _tile_std(w_gu[l][:, DFF:])
        gu = np.stack([gg, uu], axis=2).reshape(128, 44 * 1024)
        wd = w_down[l].reshape(NFF, 128, 8, 128).transpose(1, 2, 0, 3).reshape(128, 8 * DFF)
        parts = [wk.reshape(128, -1), wv, glu, wq.reshape(128, -1), wgl.reshape(128, -1),
                 _tile_std(w_pw[l]).reshape(128, -1), _tile_std(w_out[l]).reshape(128, -1), gu, wd]
        row = np.concatenate(parts, axis=1)
        assert row.shape[1] == WTOT, row.shape
        out[l] = row
    return out


def prep_params(g_mix, g_q, g_k, w_dw, b_dw, ln_g, ln_b, b_gate, g_ffn, g_final):
    par = np.zeros((128, NPAR), np.float32)

    def cols(v):
        return v.reshape(-1, 128).T
    for l in range(L):
        b = l * PL
        par[:, b + C_GMIX:b + C_GMIX + 8] = cols(g_mix[l])
        par[:, b + C_GFFN:b + C_GFFN + 8] = cols(g_ffn[l])
        par[:, b + C_LNG:b + C_LNG + 8] = cols(ln_g[l])
        par[:, b + C_LNB:b + C_LNB + 8] = cols(ln_b[l])
        par[:, b + C_BDW:b + C_BDW + 8] = cols(b_dw[l])
        par[:, b + C_BG:b + C_BG + 16] = cols(b_gate[l])
        par[:, b + C_WDW:b + C_WDW + CK * 8] = w_dw[l].reshape(CK, 8, 128).transpose(2, 0, 1).reshape(128, CK * 8)
        par[:, b + C_GQ] = g_q[l]
        par[:, b + C_GK] = g_k[l]
    par[:, C_GF:C_GF + 8] = cols(g_final)
    return par


def prep_consts():
    d = np.arange(128)
    axis = d // 64
    half = (d % 64) // 32
    i = d % 32
    inv = (ROPE_THETA ** (-(i.astype(np.float32)) / np.float32(32))).astype(np.float32)
    t = np.arange(SMAX)
    pos = np.where(axis[:, None] == 0, (t // GRID_W)[None, :], (t % GRID_W)[None, :]).astype(np.float32)
    ang = (pos * inv[:, None]).astype(np.float32)
    cosT = np.cos(ang).astype(np.float32)
    sinT = (np.sin(ang) * np.where(half == 0, -1.0, 1.0)[:, None]).astype(np.float32)
    swp = np.zeros((128, 128), np.float32)
    pi = np.where((d % 64) < 32, d + 32, d - 32)
    swp[pi, d] = 1.0
    return np.stack([cosT, sinT]).astype(np.float32), swp


def to_tiles(x2d):
    nt = x2d.shape[0] // T
    return np.ascontiguousarray(x2d.reshape(nt, T, 8, 128).transpose(0, 3, 2, 1))


def from_tiles(y):
    nt = y.shape[0]
    return np.ascontiguousarray(y.transpose(0, 3, 2, 1)).reshape(nt * T, D)


_CACHE = {}


def run_cores(seq_lists_x, weights, n_cores):
    seqs = tuple(int(a.shape[0]) for a in seq_lists_x[0])
    if seqs not in _CACHE:
        _CACHE[seqs] = build_program(list(seqs))
    nc = _CACHE[seqs]
    wall, par, cst, swp = weights
    in_maps = []
    for c in range(n_cores):
        xcat = np.concatenate(seq_lists_x[c], axis=0)
        in_maps.append({"xT": to_tiles(xcat), "wall": wall, "par": par, "cst": cst, "swp": swp})
    res = run_bass_kernel_spmd(nc, in_maps, core_ids=list(range(n_cores)))
    outs = []
    for c in range(n_cores):
        y = from_tiles(np.asarray(res.results[c]["yT"]))
        o, k = [], 0
        for S in seqs:
            o.append(y[k:k + S])
            k += S
        outs.append(o)
    return outs


def kernel(x_prompt, x_sample, g_mix, w_in, g_q, g_k, w_dw, b_dw, ln_g, ln_b,
           w_pw, b_gate, w_out, g_ffn, w_gu, w_down, g_final):
    f = lambda a: np.asarray(a, dtype=np.float32)
    x_prompt, x_sample = f(x_prompt), f(x_sample)
    wall = prep_weights(f(w_in), f(w_pw), f(w_out), f(w_gu), f(w_down))
    par = prep_params(f(g_mix), f(g_q), f(g_k), f(w_dw), f(b_dw), f(ln_g), f(ln_b), f(b_gate), f(g_ffn), f(g_final))
    cst, swp = prep_consts()
    n = 8
    seq_lists = [[x_prompt[2 * c], x_prompt[2 * c + 1], x_sample[c]] for c in range(n)]
    outs = run_cores(seq_lists, (wall, par, cst, swp), n)
    y_prompt = np.stack([outs[c][i] for c in range(n) for i in range(2)], axis=0)
    y_sample = np.stack([outs[c][2] for c in range(n)], axis=0)
    return (y_prompt.astype(np.float32), y_sample.astype(np.float32))
```

```python
import numpy as np
from contextlib import ExitStack
import concourse.bass as bass
import concourse.mybir as mybir
from concourse.bass_utils import run_bass_kernel_spmd

F32 = mybir.dt.float32
BF16 = mybir.dt.bfloat16
AF = mybir.ActivationFunctionType
ALU = mybir.AluOpType

ENGS = ("pe", "act", "dve", "pool", "sp")

D = 1024
NH = 8
NKV = 2
HD = 128
DFF = 2816
NFF = DFF // 128
CK = 31
EPS = 1e-6
L = 2
T = 512
SMAX = 4096
GRID_W = 64
ROPE_THETA = 10000.0

WA = 20 * 1024
WB = (8 + 16 + 8 + 8 + 44) * 1024 + 8 * DFF
WTOT = WA + WB
LOADE = 4096
NS = 3
NDVE_CONV = 8
CONV_A = 4
CONV_Q = 1
CONV_FFN_GU = 4
CONV_FFN_DN = 5
CONV_PROJ = 3
CONV_ATT_FRAC = 0.92

PL = 306
C_GMIX, C_GFFN, C_LNG, C_LNB, C_BDW, C_BG, C_WDW, C_GQ, C_GK = 0, 8, 16, 24, 32, 40, 56, 304, 305
C_GF = 2 * PL
NPAR = C_GF + 8


class Prog:
    def __init__(self, nc, stack, n_dma_sems=24):
        self.nc = nc
        self.q = {e: [] for e in ENGS}
        self.sem = {e: stack.enter_context(nc.semaphore("c_" + e)) for e in ENGS}
        self.cnt = {e: 0 for e in ENGS}
        self.seen = {e: {f: 0 for f in ENGS} for e in ENGS}
        self.seen_dma = {e: {} for e in ENGS}
        self.snap = {}
        self.regions = {}
        self.dma_pool = {}
        self.dma_sems = []
        for qn in ("sp", "act"):
            n = n_dma_sems if qn == "sp" else 16
            self.dma_pool[qn] = []
            for i in range(n):
                h = stack.enter_context(nc.semaphore(f"d_{qn}{i}"))
                self.dma_pool[qn].append([h, 0, len(self.dma_sems)])
                self.dma_sems.append(h)
        self.dma_rr = {qn: 0 for qn in self.dma_pool}
        self.nwaits = 0
        self.nops = {e: 0 for e in ENGS}

    def _deps(self, eng, reads, writes, is_dma):
        deps = []
        for (r, lo, hi) in reads:
            for a in self.regions.get(r, ()):
                if a[2] and a[0] < hi and lo < a[1]:
                    deps.append(a[3])
        for (r, lo, hi) in writes:
            for a in self.regions.get(r, ()):
                if a[0] < hi and lo < a[1]:
                    ev = a[3]
                    if (not is_dma) and ev[0] == "E" and ev[1] == eng:
                        continue
                    deps.append(ev)
        return deps

    def _record(self, ev, reads, writes):
        for (r, lo, hi) in writes:
            lst = self.regions.setdefault(r, [])
            lst[:] = [a for a in lst if not (lo <= a[0] and a[1] <= hi)]
            lst.append((lo, hi, True, ev))
        for (r, lo, hi) in reads:
            lst = self.regions.setdefault(r, [])
            if ev[0] == "E":
                ke = ev[1]
                lst[:] = [a for a in lst if not ((not a[2]) and a[0] == lo and a[1] == hi
                                                 and a[3][0] == "E" and a[3][1] == ke)]
            lst.append((lo, hi, False, ev))

    def _inherit(self, eng, ev):
        s = self.snap.get(ev)
        if s is None:
            return
        seen = self.seen[eng]
        for f, v in s[0].items():
            if f != eng and v > seen[f]:
                seen[f] = v
        sd = self.seen_dma[eng]
        for d, v in s[1].items():
            if v > sd.get(d, 0):
                sd[d] = v

    def _emit_waits(self, eng, deps):
        waits = []
        seen = self.seen[eng]
        sdma = self.seen_dma[eng]
        need_e, need_d = {}, {}
        for ev in deps:
            if ev[0] == "E":
                if ev[2] > need_e.get(ev[1], 0):
                    need_e[ev[1]] = ev[2]
            else:
                if ev[2] > need_d.get(ev[1], 0):
                    need_d[ev[1]] = ev[2]
        for f, v in need_e.items():
            if seen[f] >= v:
                continue
            waits.append((self.sem[f], v))
            self._inherit(eng, ("E", f, v))
            seen[f] = max(seen[f], v)
        for d, v in need_d.items():
            if sdma.get(d, 0) >= v:
                continue
            waits.append((self.dma_sems[d], v))
            sdma[d] = v
            self._inherit(eng, ("D", d, v))
        self.nwaits += len(waits)
        return waits

    def op(self, eng, fns, reads=(), writes=()):
        if callable(fns):
            fns = [fns]
        deps = self._deps(eng, reads, writes, False)
        waits = self._emit_waits(eng, deps)
        self.cnt[eng] += 1
        ev = ("E", eng, self.cnt[eng])
        self.snap[ev] = (dict(self.seen[eng]), dict(self.seen_dma[eng]))
        sem = self.sem[eng]

        def run(e, waits=waits, fns=fns, sem=sem):
            for (s, val) in waits:
                e.wait_ge(s, val)
            for f in fns[:-1]:
                f(e)
            fns[-1](e).then_inc(sem, 1)
        self.q[eng].append(run)
        self._record(ev, reads, writes)
        self.nops[eng] += len(fns)
        return ev

    def dma(self, qn, out, in_, reads=(), writes=()):
        deps = self._deps(qn, reads, writes, True)
        pool = self.dma_pool[qn]
        i = self.dma_rr[qn]
        self.dma_rr[qn] = (i + 1) % len(pool)
        slot = pool[i]
        d = slot[2]
        if slot[1] > 0:
            deps.append(("D", d, slot[1]))
        waits = self._emit_waits(qn, deps)
        slot[1] += 16
        ev = ("D", d, slot[1])
        self.snap[ev] = (dict(self.seen[qn]), dict(self.seen_dma[qn]))
        sem = slot[0]

        def run(e, waits=waits, out=out, in_=in_, sem=sem):
            for (s, val) in waits:
                e.wait_ge(s, val)
            e.dma_start(out=out, in_=in_).then_inc(sem, 16)
        self.q[qn].append(run)
        self._record(ev, reads, writes)
        self.nops[qn] += 1
        return ev

    def barrier_all(self):
        evs = [("E", f, self.cnt[f]) for f in ENGS if self.cnt[f] > 0]
        for qn, pool in self.dma_pool.items():
            for slot in pool:
                if slot[1] > 0:
                    evs.append(("D", slot[2], slot[1]))
        for e in ENGS:
            waits = self._emit_waits(e, list(evs))
            if waits:
                def run(eo, waits=waits):
                    for (s, val) in waits:
                        eo.wait_ge(s, val)
                self.q[e].append(run)
        self.regions = {}

    def run_blocks(self):
        with self.nc.Block() as block:
            @block.tensor
            def _(e):
                for f in self.q["pe"]:
                    f(e)

            @block.scalar
            def _(e):
                for f in self.q["act"]:
                    f(e)

            @block.vector
            def _(e):
                for f in self.q["dve"]:
                    f(e)

            @block.gpsimd
            def _(e):
                for f in self.q["pool"]:
                    f(e)

            @block.sync
            def _(e):
                for f in self.q["sp"]:
                    f(e)


class SB:
    def __init__(self, sb_ap):
        self.sb = sb_ap
        self.off = 0

    def alloc(self, nbytes):
        o = self.off
        self.off += (nbytes + 63) // 64 * 64
        return o

    def view(self, off, nbytes, dt, c=None):
        v = self.sb[:, off // 4:(off + nbytes) // 4]
        if dt != F32:
            v = v.bitcast(dt)
        if c is not None:
            v = v.rearrange("p (c t) -> p c t", c=c)
        return v


class T3:
    _n = [0]

    def __init__(self, sbm, off, C, W, dt, region=None, rbase=None):
        self.ds = 4 if dt == F32 else 2
        self.C, self.W = C, W
        self.ap = sbm.view(off, C * W * self.ds, dt, c=C)
        if region is None:
            T3._n[0] += 1
            region, rbase = "t%d" % T3._n[0], off
        self.region = region
        self.off = off - rbase

    def ch(self, c, a=0, b=None):
        b = self.W if b is None else b
        lo = self.off + (c * self.W + a) * self.ds
        hi = self.off + (c * self.W + b) * self.ds
        return self.ap[:, c, a:b], (self.region, lo, hi)

    def iv(self, c0=0, c1=None):
        c1 = self.C if c1 is None else c1
        return (self.region, self.off + c0 * self.W * self.ds, self.off + c1 * self.W * self.ds)


def build_program(seqs):
    NT = sum(seqs) // T
    nc = bass.Bass("TRN2", target_bir_lowering=False)
    xT = nc.dram_tensor("xT", [NT, 128, 8, T], F32, kind="ExternalInput").ap()
    wall = nc.dram_tensor("wall", [L, 128, WTOT], F32, kind="ExternalInput").ap()
    par_d = nc.dram_tensor("par", [128, NPAR], F32, kind="ExternalInput").ap()
    cst_d = nc.dram_tensor("cst", [2, 128, SMAX], F32, kind="ExternalInput").ap()
    swp_d = nc.dram_tensor("swp", [128, 128], F32, kind="ExternalInput").ap()
    yT = nc.dram_tensor("yT", [NT, 128, 8, T], F32, kind="ExternalOutput").ap()
    wsc = nc.dram_tensor("wsc", [L, 128, WTOT], BF16, kind="Internal").ap()
    xs = nc.dram_tensor("xs", [NT, 128, 8, T], F32, kind="Internal").ap()
    zs = nc.dram_tensor("zs", [NT, 128, 8, T], BF16, kind="Internal").ap()

    with ExitStack() as st:
        SBBYTES = 212480
        sb_t = st.enter_context(nc.sbuf_tensor("sb", [128, SBBYTES // 4], F32))
        ps_t = st.enter_context(nc.psum_tensor("ps", [128, 8, 512], F32))
        P = Prog(nc, st)
        M = SB(sb_t)

        def PS(b, n=512):
            return ps_t[:, b, 0:n], ("ps", b * 2048, b * 2048 + n * 4)

        o_par = M.alloc(NPAR * 4)
        par = M.view(o_par, NPAR * 4, F32)
        o_swb = M.alloc(256)
        swb = M.view(o_swb, 256, BF16)
        o_one = M.alloc(256)
        ones = M.view(o_one, 256, BF16)
        o_eps = M.alloc(64)
        epsc = M.view(o_eps, 4, F32)
        KT = T3(M, M.alloc(NKV * SMAX * 2), NKV, SMAX, BF16)
        Vt_off = M.alloc((SMAX // 128) * 256 * 2)
        Vt = T3(M, Vt_off, SMAX // 128, 256, BF16)
        o_tile = M.off
        xtb = [T3(M, M.alloc(8 * T * 4), 8, T, F32), T3(M, M.alloc(8 * T * 4), 8, T, F32)]
        XT = {"cur": xtb[0]}
        sq = T3(M, M.alloc(8 * T * 2), 8, T, BF16)
        hTb = [T3(M, M.alloc(8 * T * 2), 8, T, BF16), T3(M, M.alloc(8 * T * 2), 8, T, BF16)]
        HT = {"cur": hTb[0]}

        class _CurH:
            def ch(self, *a, **k):
                return HT["cur"].ch(*a, **k)

            def iv(self, *a, **k):
                return HT["cur"].iv(*a, **k)
        hT = _CurH()
        QT_off = M.alloc(8 * T * 2)
        QT = T3(M, QT_off, 8, T, BF16, "qg", QT_off)
        gat_off = M.alloc(16 * T * 2)
        assert gat_off == QT_off + 8 * T * 2
        gat = T3(M, gat_off, 16, T, BF16, "qg", QT_off)
        hh = T3(M, QT_off, NFF, T, BF16, "qg", QT_off)
        o_big = M.alloc(8 * T * 4 + 8 * (T + 30) * 2)
        acc = T3(M, o_big, 8, T, F32, "big", o_big)
        zb = T3(M, o_big + 8 * T * 4, 8, T + 30, BF16, "big", o_big)
        mc = T3(M, M.alloc(8 * T * 4), 8, T, F32)
        mm = T3(M, QT_off, 8, T, BF16, "qg", QT_off)
        lnt = T3(M, M.alloc(2 * T * 4), 2, T, F32)
        PT = T3(M, M.alloc(4 * T * 2), 4, T, BF16)
        qgb = T3(M, M.alloc(2 * T * 2), 2, T, BF16)
        cs = T3(M, M.alloc(2 * T * 4), 2, T, F32)
        NMISC = 5
        rsn = T3(M, M.alloc(2 * T * 4), 2, T, F32)
        misc = T3(M, M.alloc(NMISC * T * 4), NMISC, T, F32)
        ring_off = M.alloc(NS * LOADE * 2)
        ring = T3(M, ring_off, NS, LOADE, BF16)
        assert M.off <= SBBYTES, M.off

        def pcol(c):
            return par[:, c:c + 1]

        P.dma("sp", par, par_d, writes=[("sb", o_par, o_par + NPAR * 4)])
        swf_t = T3(M, ring_off, 1, 128, F32, "swf", ring_off)
        swf = swf_t.ch(0)
        P.dma("sp", swf[0], swp_d, writes=[swf[1]])
        P.op("dve", lambda e: e.tensor_copy(out=swb, in_=swf[0]), reads=[swf[1]],
             writes=[("sb", o_swb, o_swb + 256)])
        P.op("pool", lambda e: e.memset(ones, 1.0), writes=[("sb", o_one, o_one + 256)])
        P.op("pool", lambda e: e.memset(epsc, EPS), writes=[("sb", o_eps, o_eps + 4)])

        P.barrier_all()

        assert seqs[0] <= SMAX // 2
        CVP = 1024
        stg = []
        for k_ in range(2):
            o_ = (SMAX // 128) * 256 * 2 // 2 + k_ * CVP * 4
            stg.append((M.view(Vt_off + o_, CVP * 4, F32), (Vt.region, o_, o_ + CVP * 4)))
        cvt = {"i": 0, "done": set()}

        sched = []
        for S in seqs:
            for l in range(L):
                for _ in range(S // T):
                    sched.append((l, "A"))
                for _ in range(S // T):
                    sched.append((l, "B"))
        loads = []
        for (l, ph) in sched:
            if ph == "A":
                for k in range(5):
                    loads.append((l, k * LOADE, LOADE))
            else:
                base = WA
                for k in range(21):
                    loads.append((l, base + k * LOADE, LOADE))
                base += 21 * LOADE
                for k in range(8):
                    loads.append((l, base + k * DFF, DFF))
        wst = {"cur": -1, "issued": 0}

        def w_acquire():
            wst["cur"] += 1
            cur = wst["cur"]
            while wst["issued"] < min(len(loads), cur + NS):
                i = wst["issued"]
                l, off, n = loads[i]
                s = i % NS
                ap, iv = ring.ch(s, 0, n)
                wreg = ("wsc%d" % l, off * 2, (off + n) * 2)
                if (l, off) in cvt["done"]:
                    P.dma("sp", ap, wsc[l, :, off:off + n], reads=[wreg], writes=[iv])
                else:
                    cvt["done"].add((l, off))
                    for a_ in range(0, n, CVP):
                        b_ = min(n, a_ + CVP)
                        sg = stg[cvt["i"] % 2]
                        sgap = sg[0][:, 0:b_ - a_]
                        sgiv = (sg[1][0], sg[1][1], sg[1][1] + (b_ - a_) * 4)
                        P.dma("sp", sgap, wall[l, :, off + a_:off + b_], writes=[sgiv])
                        dst = ring.ch(s, a_, b_)
                        if False:
                            P.op("pool", lambda e, dst=dst, sgap=sgap: e.tensor_copy(out=dst[0], in_=sgap),
                                 reads=[sgiv], writes=[dst[1]])
                        else:
                            P.op("act", lambda e, dst=dst, sgap=sgap: e.activation(out=dst[0], in_=sgap, func=AF.Copy),
                                 reads=[sgiv], writes=[dst[1]])
                        cvt["i"] += 1
                    P.dma("sp", wsc[l, :, off:off + n], ap, reads=[iv], writes=[wreg])
                wst["issued"] += 1
            return cur % NS

        def w_tile(s, u, kc=8, width=128, off=None):
            o = u * 1024 if off is None else off
            ap, iv = ring.ch(s, o, o + kc * width)
            return ap.rearrange("p (k m) -> p k m", k=kc), iv

        gen_banks = {"i": 0}

        def gbank():
            b = gen_banks["i"] % 8
            gen_banks["i"] += 1
            return b

        misc_i = {"i": 0}

        def mtile():
            i = misc_i["i"] % NMISC
            misc_i["i"] += 1
            return misc.ch(i)

        def mm_group(out, pairs, reads):
            n = len(pairs)
            fns = [(lambda e, a=a, b=b, i=i: e.matmul(out[0], lhsT=a, rhs=b, start=(i == 0), stop=(i == n - 1)))
                   for i, (a, b) in enumerate(pairs)]
            P.op("pe", fns, reads=reads, writes=[out[1]])

        def rstd_from(ps_sum, nfeat):
            va = mtile()
            P.op("act", lambda e: e.activation(out=va[0], in_=ps_sum[0], func=AF.Ln, scale=1.0 / nfeat, bias=epsc),
                 reads=[ps_sum[1]], writes=[va[1]])
            rs = mtile()
            P.op("act", lambda e: e.activation(out=rs[0], in_=va[0], func=AF.Exp, scale=-0.5),
                 reads=[va[1]], writes=[rs[1]])
            return rs

        SQ_ENG = ("act", "pool", "act", "dve", "act", "pool", "act", "dve")

        def norm1a(xb):
            for c in range(8):
                a, o = xb.ch(c), sq.ch(c)
                if SQ_ENG[c] == "act":
                    P.op("act", lambda e, a=a, o=o: e.activation(out=o[0], in_=a[0], func=AF.Square),
                         reads=[a[1]], writes=[o[1]])
                else:
                    P.op(SQ_ENG[c], lambda e, a=a, o=o: e.tensor_tensor(out=o[0], in0=a[0], in1=a[0], op=ALU.mult),
                         reads=[a[1]], writes=[o[1]])

        def norm1b(rs):
            pb = PS(gbank())
            mm_group(pb, [(ones, sq.ch(c)[0]) for c in range(8)], reads=[sq.iv()])
            va = mtile()
            P.op("act", lambda e: e.activation(out=va[0], in_=pb[0], func=AF.Ln, scale=1.0 / D, bias=epsc),
                 reads=[pb[1]], writes=[va[1]])
            P.op("act", lambda e: e.activation(out=rs[0], in_=va[0], func=AF.Exp, scale=-0.5),
                 reads=[va[1]], writes=[rs[1]])

        def norm1(xb, rs):
            norm1a(xb)
            norm1b(rs)

        def norm2(xb, rs, gcol0, hbuf=None):
            hbuf = HT["cur"] if hbuf is None else hbuf
            for c in range(8):
                a, o = xb.ch(c), hbuf.ch(c)
                P.op("dve", lambda e, a=a, o=o, c=c: e.scalar_tensor_tensor(
                    out=o[0], in0=a[0], scalar=pcol(gcol0 + c), in1=rs[0], op0=ALU.mult, op1=ALU.mult),
                    reads=[a[1], rs[1]], writes=[o[1]])

        def rmsnorm_to_hT(gcol0):
            rs = mtile()
            norm1(XT["cur"], rs)
            norm2(XT["cur"], rs, gcol0)

        pend = {"rs": None, "h": False}

        def start_norm(gcol0):
            if pend["rs"] is None:
                rs = rsn.ch(0)
                norm1(XT["cur"], rs)
            else:
                rs = pend["rs"]
            if not pend["h"]:
                norm2(XT["cur"], rs, gcol0)
            pend["rs"], pend["h"] = None, False

        def qk_s1(pb, gcol, slot):
            qg = qgb.ch(slot)
            s2 = sq.ch(slot)
            P.op("act", lambda e: e.activation(out=qg[0], in_=pb[0], func=AF.Copy, scale=pcol(gcol)),
                 reads=[pb[1]], writes=[qg[1]])
            P.op("act", lambda e: e.activation(out=s2[0], in_=pb[0], func=AF.Square),
                 reads=[pb[1]], writes=[s2[1]])

        def qk_s2(out, slot):
            qg = qgb.ch(slot)
            s2 = sq.ch(slot)
            pss = PS(gbank())
            mm_group(pss, [(ones, s2[0])], reads=[s2[1]])
            psw = PS(gbank())
            mm_group(psw, [(swb, qg[0])], reads=[qg[1]])
            rs = rstd_from(pss, HD)
            t1, t2 = mtile(), mtile()
            cosv, sinv = cs.ch(0), cs.ch(1)
            P.op("pool", lambda e: e.tensor_tensor(out=t1[0], in0=qg[0], in1=cosv[0], op=ALU.mult),
                 reads=[qg[1], cosv[1]], writes=[t1[1]])
            P.op("dve", lambda e: e.tensor_tensor(out=t2[0], in0=psw[0], in1=sinv[0], op=ALU.mult),
                 reads=[psw[1], sinv[1]], writes=[t2[1]])
            P.op("pool", lambda e: e.tensor_tensor(out=t1[0], in0=t1[0], in1=t2[0], op=ALU.add),
                 reads=[t1[1], t2[1]], writes=[t1[1]])
            P.op("dve", lambda e: e.tensor_tensor(out=out[0], in0=t1[0], in1=rs[0], op=ALU.mult),
                 reads=[t1[1], rs[1]], writes=[out[1]])

        XQ = {"q": "act"}
        IDX = {"i": 0}
        DEFER = []

        def flush_deferred():
            while DEFER:
                DEFER.pop(0)()

        def load_x(l, g, xb):
            src = xT if l == 0 else xs
            rd = [] if l == 0 else [("xs", g, g + 1)]
            P.dma(XQ["q"], xb.ap, src[g], reads=rd, writes=[xb.iv()])

        def load_cs(tpos):
            for i in range(2):
                ap, iv = cs.ch(i)
                P.dma("act", ap, cst_d[i, :, tpos:tpos + T], writes=[iv])

        def phase_a(l, g, ti, nt, nxt, xnext):
            pc = l * PL
            xt = XT["cur"]
            load_cs(ti * T)
            start_norm(pc + C_GMIX)
            s = w_acquire()
            wv, wviv = w_tile(s, 2, kc=8, width=256)

            def vblk(blk):
                pb = PS(gbank(), 256)
                mm_group(pb, [(hT.ch(k, blk * 128, (blk + 1) * 128)[0], wv[:, k, :]) for k in range(8)],
                         reads=[wviv, hT.iv()])
                o = Vt.ch(ti * 4 + blk)
                P.op("act", lambda e, o=o, pb=pb: e.activation(out=o[0], in_=pb[0], func=AF.Copy),
                     reads=[pb[1]], writes=[o[1]])
            for j in range(NKV):
                w, wiv = w_tile(s, j)
                pb = PS(gbank())
                mm_group(pb, [(w[:, k, :], hT.ch(k)[0]) for k in range(8)], reads=[wiv, hT.iv()])
                qk_s1(pb, pc + C_GK, j)
            flush_deferred()
            if nxt is not None:
                load_x(nxt[0], nxt[3], xnext)
            g_b0 = g - ti
            pre = ti >= 2
            if pre and BG["for"] != (l, g_b0):
                load_zb(g_b0, 0, nt)
                build_bg(l, g_b0, hTb[(IDX["i"] + nt - ti) % 2])
            vblk(0)
            vblk(1)
            qk_s2(KT.ch(0, ti * T, (ti + 1) * T), 0)
            vblk(2)
            qk_s2(KT.ch(1, ti * T, (ti + 1) * T), 1)
            if nxt is not None:
                pend["rs"] = rsn.ch(1 - pend.get("k", 0))
                pend["k"] = 1 - pend.get("k", 0)
                norm1a(xnext)
            vblk(3)
            for c in range(8):
                if nxt is not None and c == 1:
                    norm1b(pend["rs"])
                if nxt is not None and c == 2:
                    norm2(xnext, pend["rs"], nxt[0] * PL + C_GMIX, HT["next"])
                    pend["h"] = True
                if c % 2 == 0:
                    s = w_acquire()
                wa, waiv = w_tile(s, (c % 2) * 2)
                wg, wgiv = w_tile(s, (c % 2) * 2 + 1)
                pa, pg = PS(gbank()), PS(gbank())
                mm_group(pa, [(wa[:, k, :], hT.ch(k)[0]) for k in range(8)], reads=[waiv, hT.iv()])
                mm_group(pg, [(wg[:, k, :], hT.ch(k)[0]) for k in range(8)], reads=[wgiv, hT.iv()])
                sg = mtile()
                P.op("act", lambda e, sg=sg, pg=pg: e.activation(out=sg[0], in_=pg[0], func=AF.Sigmoid),
                     reads=[pg[1]], writes=[sg[1]])
                o = QT.ch(c)
                P.op("dve", lambda e, o=o, pa=pa, sg=sg: e.tensor_tensor(out=o[0], in0=pa[0], in1=sg[0], op=ALU.mult),
                     reads=[pa[1], sg[1]], writes=[o[1]])
                if pre:
                    conv_step(CONV_A, True)
            DEFER.append(lambda: P.dma("act", zs[g], QT.ap, reads=[QT.iv()], writes=[("zs", g, g + 1)]))

        BG = {"q": [], "i": 0, "ntap": 0, "for": None}

        def load_zb(g, ti, nt):
            P.dma("sp", zb.ap[:, :, 15:15 + T], zs[g], reads=[("zs", g, g + 1)], writes=[zb.iv()])
            if ti > 0:
                P.dma("sp", zb.ap[:, :, 0:15], zs[g - 1, :, :, T - 15:T], reads=[("zs", g - 1, g)], writes=[zb.iv()])
            else:
                P.op("pool", lambda e: e.memset(zb.ap[:, :, 0:15], 0.0), writes=[zb.iv()])
            if ti < nt - 1:
                P.dma("sp", zb.ap[:, :, 15 + T:30 + T], zs[g + 1, :, :, 0:15], reads=[("zs", g + 1, g + 2)],
                      writes=[zb.iv()])
            else:
                P.op("pool", lambda e: e.memset(zb.ap[:, :, 15 + T:30 + T], 0.0), writes=[zb.iv()])

        def build_bg(l, g, hT):
            pc = l * PL
            bg = []

            def tap_dve(k, c):
                zi, a = zb.ch(c, k, k + T), acc.ch(c)
                wcol = pcol(pc + C_WDW + k * 8 + c)
                if k == 0:
                    return lambda: P.op("dve", lambda e: e.tensor_scalar(
                        out=a[0], in0=zi[0], scalar1=wcol, scalar2=pcol(pc + C_BDW + c),
                        op0=ALU.mult, op1=ALU.add), reads=[zi[1]], writes=[a[1]])
                return lambda: P.op("dve", lambda e: e.scalar_tensor_tensor(
                    out=a[0], in0=zi[0], scalar=wcol, in1=a[0], op0=ALU.mult, op1=ALU.add),
                    reads=[zi[1], a[1]], writes=[a[1]])

            for k in range(CK):
                for c in range(8):
                    bg.append(tap_dve(k, c))
            ntap = len(bg)
            mu, lrs = lnt.ch(0), lnt.ch(1)

            def ln_evac(c):
                a, o1, o2 = acc.ch(c), hT.ch(c), sq.ch(c)

                def f():
                    P.op("act", lambda e: e.activation(out=o1[0], in_=a[0], func=AF.Copy), reads=[a[1]], writes=[o1[1]])
                    P.op("act", lambda e: e.activation(out=o2[0], in_=a[0], func=AF.Square), reads=[a[1]], writes=[o2[1]])
                return f
            for c in range(8):
                bg.append(ln_evac(c))
            bg.extend([None] * 12)
            pm = PS(3)
            bg.append(lambda: mm_group(pm, [(ones, hT.ch(c)[0]) for c in range(8)], reads=[hT.iv()]))
            bg.extend([None] * 4)
            bg.append(lambda: P.op("dve", lambda e: e.tensor_scalar(out=mu[0], in0=pm[0], scalar1=1.0 / D, scalar2=None,
                                                                     op0=ALU.mult), reads=[pm[1]], writes=[mu[1]]))
            bg.append(lambda: mm_group(pm, [(ones, sq.ch(c)[0]) for c in range(8)], reads=[sq.iv()]))
            bg.extend([None] * 4)

            def ln_rstd():
                vv = mtile()
                P.op("dve", lambda e: e.tensor_tensor(out=vv[0], in0=mu[0], in1=mu[0], op=ALU.mult),
                     reads=[mu[1]], writes=[vv[1]])
                P.op("dve", lambda e: e.scalar_tensor_tensor(out=vv[0], in0=pm[0], scalar=1.0 / D, in1=vv[0],
                                                             op0=ALU.mult, op1=ALU.subtract),
                     reads=[pm[1], vv[1]], writes=[vv[1]])
                P.op("act", lambda e: e.activation(out=vv[0], in_=vv[0], func=AF.Ln, bias=epsc),
                     reads=[vv[1]], writes=[vv[1]])
                P.op("act", lambda e: e.activation(out=lrs[0], in_=vv[0], func=AF.Exp, scale=-0.5),
                     reads=[vv[1]], writes=[lrs[1]])
            bg.append(ln_rstd)

            def ln_sub(c):
                a = acc.ch(c)
                return lambda: P.op("dve", lambda e: e.tensor_tensor(out=a[0], in0=a[0], in1=mu[0], op=ALU.subtract),
                                    reads=[a[1], mu[1]], writes=[a[1]])

            def ln_mul(c):
                a = acc.ch(c)
                return lambda: P.op("dve", lambda e: e.tensor_tensor(out=a[0], in0=a[0], in1=lrs[0], op=ALU.mult),
                                    reads=[a[1], lrs[1]], writes=[a[1]])

            tts = {}

            def ln_aff(c):
                a = acc.ch(c)
                return lambda: P.op("dve", lambda e: e.tensor_scalar(
                    out=a[0], in0=a[0], scalar1=pcol(pc + C_LNG + c), scalar2=pcol(pc + C_LNB + c),
                    op0=ALU.mult, op1=ALU.add), reads=[a[1]], writes=[a[1]])

            def ln_tanh(c):
                a = acc.ch(c)

                def f():
                    tts[c] = mtile()
                    tt = tts[c]
                    P.op("act", lambda e: e.activation(out=tt[0], in_=a[0], func=AF.Tanh, scale=0.5),
                         reads=[a[1]], writes=[tt[1]])
                return f

            def ln_out(c):
                a, o = acc.ch(c), hT.ch(c)

                def f():
                    tt = tts[c]
                    P.op("dve", lambda e: e.scalar_tensor_tensor(out=o[0], in0=tt[0], scalar=1.0, in1=a[0],
                                                                 op0=ALU.add, op1=ALU.mult),
                         reads=[tt[1], a[1]], writes=[o[1]])
                return f
            for c in range(8):
                bg.append(ln_sub(c))
            for c in range(8):
                bg.append(ln_mul(c))
            for c in range(8):
                bg.append(ln_aff(c))
            for c in range(10):
                if c < 8:
                    bg.append(ln_tanh(c))
                if c >= 2:
                    bg.append(ln_out(c - 2))
            BG["q"], BG["i"], BG["ntap"], BG["for"] = bg, 0, ntap, (l, g)

        def conv_step(n, taps_only=False):
            lim = BG["ntap"] if taps_only else len(BG["q"])
            while n > 0 and BG["i"] < lim:
                f = BG["q"][BG["i"]]
                BG["i"] += 1
                n -= 1
                if f is not None:
                    f()

        def phase_b(l, g, ti, nt, nxt, xnext):
            pc = l * PL
            S = nt * T
            nkc = S // 128
            xt = XT["cur"]
            load_cs(ti * T)
            if ti == 0:
                flush_deferred()
            if BG["for"] != (l, g):
                flush_deferred()
                load_zb(g, ti, nt)
                build_bg(l, g, HT["cur"])
            start_norm(pc + C_GMIX)
            conv_ops = BG["q"]
            cst_ = BG
            def gate_grp(c, s):
                w, wiv = w_tile(s, c % 4)
                pb = PS(gbank())
                mm_group(pb, [(w[:, k, :], hT.ch(k)[0]) for k in range(8)], reads=[wiv, hT.iv()])
                o = gat.ch(c)
                P.op("act", lambda e, o=o, pb=pb, c=c: e.activation(out=o[0], in_=pb[0], func=AF.Sigmoid,
                                                                  bias=pcol(pc + C_BG + c)),
                     reads=[pb[1]], writes=[o[1]])

            for h in range(NH):
                if h % 4 == 0:
                    s = w_acquire()
                w, wiv = w_tile(s, h % 4)
                pb = PS(gbank())
                mm_group(pb, [(w[:, k, :], hT.ch(k)[0]) for k in range(8)], reads=[wiv, hT.iv()])
                qk_s1(pb, pc + C_GQ, h % 2)
                if h == 2:
                    flush_deferred()
                conv_step(CONV_Q, True)
                if h >= 1:
                    qk_s2(QT.ch(h - 1), (h - 1) % 2)
            qk_s2(QT.ch(NH - 1), (NH - 1) % 2)
            for c in range(16):
                if c % 4 == 0:
                    s = w_acquire()
                gate_grp(c, s)
                conv_step(CONV_PROJ, True)
            items = [(h, kc) for h in range(NH) for kc in range(nkc)]
            LA = 2
            scale = float(HD) ** -0.5

            def head_epilogue(h):
                par_ = h % 2
                pv, sm = PS(4 + 2 * par_), PS(5 + 2 * par_)
                rc, an = mtile(), mtile()
                P.op("act", lambda e: e.activation(out=rc[0], in_=sm[0], func=AF.Ln), reads=[sm[1]], writes=[rc[1]])
                P.op("act", lambda e: e.activation(out=rc[0], in_=rc[0], func=AF.Exp, scale=-1.0), reads=[rc[1]], writes=[rc[1]])
                P.op("dve", lambda e: e.tensor_tensor(out=an[0], in0=pv[0], in1=rc[0], op=ALU.mult),
                     reads=[pv[1], rc[1]], writes=[an[1]])
                ga, o = gat.ch(h), mc.ch(h)
                P.op("pool", lambda e: e.tensor_tensor(out=o[0], in0=an[0], in1=ga[0], op=ALU.mult),
                     reads=[an[1], ga[1]], writes=[o[1]])

            n_it = len(items)
            for i in range(n_it + LA):
                if i < n_it:
                    h, kc = items[i]
                    j = h // (NH // NKV)
                    pb = PS(i % 3)
                    kt = KT.ch(j, kc * 128, (kc + 1) * 128)
                    q = QT.ch(h)
                    mm_group(pb, [(kt[0], q[0])], reads=[kt[1], q[1]])
                    pt = PT.ch(i % 4)
                    P.op("act", lambda e, pt=pt, pb=pb: e.activation(out=pt[0], in_=pb[0], func=AF.Exp, scale=scale),
                         reads=[pb[1]], writes=[pt[1]])
                    if i == 4 and nxt is not None:
                        load_x(nxt[0], nxt[3], xnext)
                    rem = len(conv_ops) - cst_["i"]
                    if rem > 0:
                        left = max(1, int(n_it * CONV_ATT_FRAC) - i)
                        conv_step(-(-rem // left))
                if i >= LA:
                    ii = i - LA
                    h, kc = items[ii]
                    j = h // (NH // NKV)
                    par_ = h % 2
                    pv, sm = PS(4 + 2 * par_), PS(5 + 2 * par_)
                    pt = PT.ch(ii % 4)
                    va, viv = Vt.ch(kc, j * 128, (j + 1) * 128)
                    first, last = (kc == 0), (kc == nkc - 1)
                    fns = [lambda e, pv=pv, va=va, pt=pt, first=first, last=last:
                           e.matmul(pv[0], lhsT=va, rhs=pt[0], start=first, stop=last),
                           lambda e, sm=sm, pt=pt, first=first, last=last:
                           e.matmul(sm[0], lhsT=ones, rhs=pt[0], start=first, stop=last)]
                    P.op("pe", fns, reads=[viv, pt[1]], writes=[pv[1], sm[1]])
                    if last:
                        head_epilogue(h)
            conv_step(len(conv_ops))
            for c in range(8):
                if c % 4 == 0:
                    s = w_acquire()
                w, wiv = w_tile(s, c % 4)
                pb = PS(gbank())
                mm_group(pb, [(w[:, k, :], hT.ch(k)[0]) for k in range(8)], reads=[wiv, hT.iv()])
                gc, tm, ma, o = gat.ch(8 + c), mtile(), mc.ch(c), mm.ch(c)
                P.op("dve", lambda e, tm=tm, pb=pb, gc=gc: e.scalar_tensor_tensor(
                    out=tm[0], in0=pb[0], scalar=0.5, in1=gc[0], op0=ALU.mult, op1=ALU.mult),
                    reads=[pb[1], gc[1]], writes=[tm[1]])
                P.op("pool", lambda e, tm=tm, ma=ma, o=o: e.tensor_tensor(out=o[0], in0=tm[0], in1=ma[0], op=ALU.add),
                     reads=[tm[1], ma[1]], writes=[o[1]])
            for c in range(8):
                if c % 4 == 0:
                    s = w_acquire()
                w, wiv = w_tile(s, c % 4)
                pb = PS(gbank())
                mm_group(pb, [(w[:, k, :], mm.ch(k)[0]) for k in range(8)], reads=[wiv, mm.iv()])
                x_ = xt.ch(c)
                P.op("dve", lambda e, x_=x_, pb=pb: e.tensor_tensor(out=x_[0], in0=pb[0], in1=x_[0], op=ALU.add),
                     reads=[pb[1], x_[1]], writes=[x_[1]])
                o_ = sq.ch(c)
                P.op("act", lambda e, x_=x_, o_=o_: e.activation(out=o_[0], in_=x_[0], func=AF.Square),
                     reads=[x_[1]], writes=[o_[1]])
            pipe = nxt is not None and nxt[1] == "B" and nxt[0] == l
            if pipe:
                load_zb(nxt[3], nxt[2], nt)
                build_bg(l, nxt[3], HT["next"])
            rs_f = mtile()
            norm1b(rs_f)
            norm2(xt, rs_f, pc + C_GFFN)
            for jf in range(NFF):
                if jf % 2 == 0:
                    s = w_acquire()
                wg, wgiv = w_tile(s, (jf % 2) * 2)
                wu, wuiv = w_tile(s, (jf % 2) * 2 + 1)
                pg, pu = PS(gbank()), PS(gbank())
                mm_group(pg, [(wg[:, k, :], hT.ch(k)[0]) for k in range(8)], reads=[wgiv, hT.iv()])
                mm_group(pu, [(wu[:, k, :], hT.ch(k)[0]) for k in range(8)], reads=[wuiv, hT.iv()])
                sg = mtile()
                P.op("act", lambda e, sg=sg, pg=pg: e.activation(out=sg[0], in_=pg[0], func=AF.Silu),
                     reads=[pg[1]], writes=[sg[1]])
                o = hh.ch(jf)
                P.op("dve", lambda e, o=o, pu=pu, sg=sg: e.tensor_tensor(out=o[0], in0=pu[0], in1=sg[0], op=ALU.mult),
                     reads=[pu[1], sg[1]], writes=[o[1]])
                if pipe:
                    conv_step(CONV_FFN_GU, True)
                if jf == 6 and nxt is not None:
                    norm1a(xnext)
            if nxt is not None:
                pend["rs"] = rsn.ch(1 - pend.get("k", 0))
                pend["k"] = 1 - pend.get("k", 0)
                norm1b(pend["rs"])
                norm2(xnext, pend["rs"], nxt[0] * PL + C_GMIX, HT["next"])
                pend["h"] = True
            for c in range(8):
                s = w_acquire()
                w, wiv = w_tile(s, 0, kc=NFF, width=128, off=0)
                pb = PS(gbank())
                mm_group(pb, [(w[:, k, :], hh.ch(k)[0]) for k in range(NFF)], reads=[wiv, hh.iv()])
                x_ = xt.ch(c)
                P.op("dve", lambda e, x_=x_, pb=pb: e.tensor_tensor(out=x_[0], in0=pb[0], in1=x_[0], op=ALU.add),
                     reads=[pb[1], x_[1]], writes=[x_[1]])
                if pipe:
                    conv_step(CONV_FFN_DN, True)
            if l < L - 1:
                DEFER.append(lambda: P.dma("act", xs[g], xt.ap, reads=[xt.iv()], writes=[("xs", g, g + 1)]))
            else:
                for c in range(8):
                    a, o = xt.ch(c), sq.ch(c)
                    P.op("act", lambda e, a=a, o=o: e.activation(out=o[0], in_=a[0], func=AF.Square),
                         reads=[a[1]], writes=[o[1]])
                pb = PS(gbank())
                mm_group(pb, [(ones, sq.ch(c)[0]) for c in range(8)], reads=[sq.iv()])
                rs = rstd_from(pb, D)
                for c in range(8):
                    a, o = xt.ch(c), mc.ch(c)
                    P.op("dve", lambda e, a=a, o=o, c=c: e.scalar_tensor_tensor(
                        out=o[0], in0=a[0], scalar=pcol(C_GF + c), in1=rs[0], op0=ALU.mult, op1=ALU.mult),
                        reads=[a[1], rs[1]], writes=[o[1]])
                DEFER.append(lambda: P.dma("act", yT[g], mc.ap, reads=[mc.iv()], writes=[("yT", g, g + 1)]))

        entries = []
        g0 = 0
        for S in seqs:
            nt = S // T
            for l in range(L):
                for ti in range(nt):
                    entries.append((l, "A", ti, g0 + ti, nt))
                for ti in range(nt):
                    entries.append((l, "B", ti, g0 + ti, nt))
            g0 += nt
        load_x(entries[0][0], entries[0][3], xtb[0])
        for i, (l, ph, ti, g, nt) in enumerate(entries):
            XT["cur"] = xtb[i % 2]
            IDX["i"] = i
            HT["cur"], HT["next"] = hTb[i % 2], hTb[(i + 1) % 2]
            nxt = entries[i + 1] if i + 1 < len(entries) else None
            if ph == "A":
                phase_a(l, g, ti, nt, nxt, xtb[(i + 1) % 2])
            else:
                phase_b(l, g, ti, nt, nxt, xtb[(i + 1) % 2])
        flush_deferred()
        assert wst["cur"] == len(loads) - 1, (wst, len(loads))
        P.barrier_all()
        P.run_blocks()
        build_program.stats = (dict(P.nops), P.nwaits)
    return nc


def _tile_std(W):
    K, Mo = W.shape
    kc, mj = K // 128, Mo // 128
    return W.reshape(kc, 128, mj, 128).transpose(1, 2, 0, 3).reshape(128, mj, kc * 128)


def prep_weights(w_in, w_pw, w_out, w_gu, w_down):
    out = np.empty((L, 128, WTOT), np.float32)
    for l in range(L):
        wi = w_in[l]
        wq = _tile_std(wi[:, 0:1024])
        wk = _tile_std(wi[:, 1024:1280])
        wv = wi[:, 1280:1536].reshape(8, 128, 256).transpose(1, 0, 2).reshape(128, 2048)
        wa = _tile_std(wi[:, 1536:2560])
        wg = _tile_std(wi[:, 2560:3584])
        wgl = _tile_std(wi[:, 3584:5632])
        glu = np.stack([wa, wg], axis=2).reshape(128, 16 * 1024)
        gg = _tile_std(w_gu[l][:, :DFF])
        uu = _tile_std(w_gu[l][:, DFF:])
        gu = np.stack([gg, uu], axis=2).reshape(128, 44 * 1024)
        wd = w_down[l].reshape(NFF, 128, 8, 128).transpose(1, 2, 0, 3).reshape(128, 8 * DFF)
        parts = [wk.reshape(128, -1), wv, glu, wq.reshape(128, -1), wgl.reshape(128, -1),
                 _tile_std(w_pw[l]).reshape(128, -1), _tile_std(w_out[l]).reshape(128, -1), gu, wd]
        row = np.concatenate(parts, axis=1)
        assert row.shape[1] == WTOT, row.shape
        out[l] = row
    return out


def prep_params(g_mix, g_q, g_k, w_dw, b_dw, ln_g, ln_b, b_gate, g_ffn, g_final):
    par = np.zeros((128, NPAR), np.float32)

    def cols(v):
        return v.reshape(-1, 128).T
    for l in range(L):
        b = l * PL
        par[:, b + C_GMIX:b + C_GMIX + 8] = cols(g_mix[l])
        par[:, b + C_GFFN:b + C_GFFN + 8] = cols(g_ffn[l])
        par[:, b + C_LNG:b + C_LNG + 8] = cols(ln_g[l])
        par[:, b + C_LNB:b + C_LNB + 8] = cols(ln_b[l])
        par[:, b + C_BDW:b + C_BDW + 8] = cols(b_dw[l])
        par[:, b + C_BG:b + C_BG + 16] = cols(b_gate[l])
        par[:, b + C_WDW:b + C_WDW + CK * 8] = w_dw[l].reshape(CK, 8, 128).transpose(2, 0, 1).reshape(128, CK * 8)
        par[:, b + C_GQ] = g_q[l]
        par[:, b + C_GK] = g_k[l]
    par[:, C_GF:C_GF + 8] = cols(g_final)
    return par


def prep_consts():
    d = np.arange(128)
    axis = d // 64
    half = (d % 64) // 32
    i = d % 32
    inv = (ROPE_THETA ** (-(i.astype(np.float32)) / np.float32(32))).astype(np.float32)
    t = np.arange(SMAX)
    pos = np.where(axis[:, None] == 0, (t // GRID_W)[None, :], (t % GRID_W)[None, :]).astype(np.float32)
    ang = (pos * inv[:, None]).astype(np.float32)
    cosT = np.cos(ang).astype(np.float32)
    sinT = (np.sin(ang) * np.where(half == 0, -1.0, 1.0)[:, None]).astype(np.float32)
    swp = np.zeros((128, 128), np.float32)
    pi = np.where((d % 64) < 32, d + 32, d - 32)
    swp[pi, d] = 1.0
    return np.stack([cosT, sinT]).astype(np.float32), swp


def to_tiles(x2d):
    nt = x2d.shape[0] // T
    return np.ascontiguousarray(x2d.reshape(nt, T, 8, 128).transpose(0, 3, 2, 1))


def from_tiles(y):
    nt = y.shape[0]
    return np.ascontiguousarray(y.transpose(0, 3, 2, 1)).reshape(nt * T, D)


_CACHE = {}


def run_cores(seq_lists_x, weights, n_cores):
    seqs = tuple(int(a.shape[0]) for a in seq_lists_x[0])
    if seqs not in _CACHE:
        _CACHE[seqs] = build_program(list(seqs))
    nc = _CACHE[seqs]
    wall, par, cst, swp = weights
    in_maps = []
    for c in range(n_cores):
        xcat = np.concatenate(seq_lists_x[c], axis=0)
        in_maps.append({"xT": to_tiles(xcat), "wall": wall, "par": par, "cst": cst, "swp": swp})
    res = run_bass_kernel_spmd(nc, in_maps, core_ids=list(range(n_cores)))
    outs = []
    for c in range(n_cores):
        y = from_tiles(np.asarray(res.results[c]["yT"]))
        o, k = [], 0
        for S in seqs:
            o.append(y[k:k + S])
            k += S
        outs.append(o)
    return outs


def kernel(x_prompt, x_sample, g_mix, w_in, g_q, g_k, w_dw, b_dw, ln_g, ln_b,
           w_pw, b_gate, w_out, g_ffn, w_gu, w_down, g_final):
    f = lambda a: np.asarray(a, dtype=np.float32)
    x_prompt, x_sample = f(x_prompt), f(x_sample)
    wall = prep_weights(f(w_in), f(w_pw), f(w_out), f(w_gu), f(w_down))
    par = prep_params(f(g_mix), f(g_q), f(g_k), f(w_dw), f(b_dw), f(ln_g), f(ln_b), f(b_gate), f(g_ffn), f(g_final))
    cst, swp = prep_consts()
    n = 8
    seq_lists = [[x_prompt[2 * c], x_prompt[2 * c + 1], x_sample[c]] for c in range(n)]
    outs = run_cores(seq_lists, (wall, par, cst, swp), n)
    y_prompt = np.stack([outs[c][i] for c in range(n) for i in range(2)], axis=0)
    y_sample = np.stack([outs[c][2] for c in range(n)], axis=0)
    return (y_prompt.astype(np.float32), y_sample.astype(np.float32))
```

```python
import numpy as np
from contextlib import ExitStack
import concourse.bass as bass
import concourse.mybir as mybir
from concourse.bass_utils import run_bass_kernel_spmd

F32 = mybir.dt.float32
BF16 = mybir.dt.bfloat16
AF = mybir.ActivationFunctionType
ALU = mybir.AluOpType

ENGS = ("pe", "act", "dve", "pool", "sp")

D = 1024
NH = 8
NKV = 2
HD = 128
DFF = 2816
NFF = DFF // 128
CK = 31
EPS = 1e-6
L = 2
T = 512
SMAX = 4096
GRID_W = 64
ROPE_THETA = 10000.0

WA = 20 * 1024
WB = (8 + 16 + 8 + 8 + 44) * 1024 + 8 * DFF
WTOT = WA + WB
LOADE = 4096
NS = 3
NDVE_CONV = 8
WARM_A = 16
WARM_B = 48
CONV_A = 4
CONV_Q = 1
CONV_FFN_GU = 4
CONV_FFN_DN = 5
CONV_PROJ = 3
CONV_ATT_FRAC = 0.92

PL = 306
C_GMIX, C_GFFN, C_LNG, C_LNB, C_BDW, C_BG, C_WDW, C_GQ, C_GK = 0, 8, 16, 24, 32, 40, 56, 304, 305
C_GF = 2 * PL
NPAR = C_GF + 8


class Prog:
    def __init__(self, nc, stack, n_dma_sems=24):
        self.nc = nc
        self.q = {e: [] for e in ENGS}
        self.sem = {e: stack.enter_context(nc.semaphore("c_" + e)) for e in ENGS}
        self.cnt = {e: 0 for e in ENGS}
        self.seen = {e: {f: 0 for f in ENGS} for e in ENGS}
        self.seen_dma = {e: {} for e in ENGS}
        self.snap = {}
        self.regions = {}
        self.dma_pool = {}
        self.dma_sems = []
        for qn in ("sp", "act"):
            n = n_dma_sems if qn == "sp" else 16
            self.dma_pool[qn] = []
            for i in range(n):
                h = stack.enter_context(nc.semaphore(f"d_{qn}{i}"))
                self.dma_pool[qn].append([h, 0, len(self.dma_sems)])
                self.dma_sems.append(h)
        self.dma_rr = {qn: 0 for qn in self.dma_pool}
        self.nwaits = 0
        self.nops = {e: 0 for e in ENGS}

    def _deps(self, eng, reads, writes, is_dma):
        deps = []
        for (r, lo, hi) in reads:
            for a in self.regions.get(r, ()):
                if a[2] and a[0] < hi and lo < a[1]:
                    deps.append(a[3])
        for (r, lo, hi) in writes:
            for a in self.regions.get(r, ()):
                if a[0] < hi and lo < a[1]:
                    ev = a[3]
                    if (not is_dma) and ev[0] == "E" and ev[1] == eng:
                        continue
                    deps.append(ev)
        return deps

    def _record(self, ev, reads, writes):
        for (r, lo, hi) in writes:
            lst = self.regions.setdefault(r, [])
            lst[:] = [a for a in lst if not (lo <= a[0] and a[1] <= hi)]
            lst.append((lo, hi, True, ev))
        for (r, lo, hi) in reads:
            lst = self.regions.setdefault(r, [])
            if ev[0] == "E":
                ke = ev[1]
                lst[:] = [a for a in lst if not ((not a[2]) and a[0] == lo and a[1] == hi
                                                 and a[3][0] == "E" and a[3][1] == ke)]
            lst.append((lo, hi, False, ev))

    def _inherit(self, eng, ev):
        s = self.snap.get(ev)
        if s is None:
            return
        seen = self.seen[eng]
        for f, v in s[0].items():
            if f != eng and v > seen[f]:
                seen[f] = v
        sd = self.seen_dma[eng]
        for d, v in s[1].items():
            if v > sd.get(d, 0):
                sd[d] = v

    def _emit_waits(self, eng, deps):
        waits = []
        seen = self.seen[eng]
        sdma = self.seen_dma[eng]
        need_e, need_d = {}, {}
        for ev in deps:
            if ev[0] == "E":
                if ev[2] > need_e.get(ev[1], 0):
                    need_e[ev[1]] = ev[2]
            else:
                if ev[2] > need_d.get(ev[1], 0):
                    need_d[ev[1]] = ev[2]
        for f, v in need_e.items():
            if seen[f] >= v:
                continue
            waits.append((self.sem[f], v))
            self._inherit(eng, ("E", f, v))
            seen[f] = max(seen[f], v)
        for d, v in need_d.items():
            if sdma.get(d, 0) >= v:
                continue
            waits.append((self.dma_sems[d], v))
            sdma[d] = v
            self._inherit(eng, ("D", d, v))
        self.nwaits += len(waits)
        return waits

    def op(self, eng, fns, reads=(), writes=()):
        if callable(fns):
            fns = [fns]
        deps = self._deps(eng, reads, writes, False)
        waits = self._emit_waits(eng, deps)
        self.cnt[eng] += 1
        ev = ("E", eng, self.cnt[eng])
        self.snap[ev] = (dict(self.seen[eng]), dict(self.seen_dma[eng]))
        sem = self.sem[eng]

        def run(e, waits=waits, fns=fns, sem=sem):
            for (s, val) in waits:
                e.wait_ge(s, val)
            for f in fns[:-1]:
                f(e)
            fns[-1](e).then_inc(sem, 1)
        self.q[eng].append(run)
        self._record(ev, reads, writes)
        self.nops[eng] += len(fns)
        return ev

    def dma(self, qn, out, in_, reads=(), writes=()):
        deps = self._deps(qn, reads, writes, True)
        pool = self.dma_pool[qn]
        i = self.dma_rr[qn]
        self.dma_rr[qn] = (i + 1) % len(pool)
        slot = pool[i]
        d = slot[2]
        if slot[1] > 0:
            deps.append(("D", d, slot[1]))
        waits = self._emit_waits(qn, deps)
        slot[1] += 16
        ev = ("D", d, slot[1])
        self.snap[ev] = (dict(self.seen[qn]), dict(self.seen_dma[qn]))
        sem = slot[0]

        def run(e, waits=waits, out=out, in_=in_, sem=sem):
            for (s, val) in waits:
                e.wait_ge(s, val)
            e.dma_start(out=out, in_=in_).then_inc(sem, 16)
        self.q[qn].append(run)
        self._record(ev, reads, writes)
        self.nops[qn] += 1
        return ev

    def barrier_all(self):
        evs = [("E", f, self.cnt[f]) for f in ENGS if self.cnt[f] > 0]
        for qn, pool in self.dma_pool.items():
            for slot in pool:
                if slot[1] > 0:
                    evs.append(("D", slot[2], slot[1]))
        for e in ENGS:
            waits = self._emit_waits(e, list(evs))
            if waits:
                def run(eo, waits=waits):
                    for (s, val) in waits:
                        eo.wait_ge(s, val)
                self.q[e].append(run)
        self.regions = {}

    def run_blocks(self):
        with self.nc.Block() as block:
            @block.tensor
            def _(e):
                for f in self.q["pe"]:
                    f(e)

            @block.scalar
            def _(e):
                for f in self.q["act"]:
                    f(e)

            @block.vector
            def _(e):
                for f in self.q["dve"]:
                    f(e)

            @block.gpsimd
            def _(e):
                for f in self.q["pool"]:
                    f(e)

            @block.sync
            def _(e):
                for f in self.q["sp"]:
                    f(e)


class SB:
    def __init__(self, sb_ap):
        self.sb = sb_ap
        self.off = 0

    def alloc(self, nbytes):
        o = self.off
        self.off += (nbytes + 63) // 64 * 64
        return o

    def view(self, off, nbytes, dt, c=None):
        v = self.sb[:, off // 4:(off + nbytes) // 4]
        if dt != F32:
            v = v.bitcast(dt)
        if c is not None:
            v = v.rearrange("p (c t) -> p c t", c=c)
        return v


class T3:
    _n = [0]

    def __init__(self, sbm, off, C, W, dt, region=None, rbase=None):
        self.ds = 4 if dt == F32 else 2
        self.C, self.W = C, W
        self.ap = sbm.view(off, C * W * self.ds, dt, c=C)
        if region is None:
            T3._n[0] += 1
            region, rbase = "t%d" % T3._n[0], off
        self.region = region
        self.off = off - rbase

    def ch(self, c, a=0, b=None):
        b = self.W if b is None else b
        lo = self.off + (c * self.W + a) * self.ds
        hi = self.off + (c * self.W + b) * self.ds
        return self.ap[:, c, a:b], (self.region, lo, hi)

    def iv(self, c0=0, c1=None):
        c1 = self.C if c1 is None else c1
        return (self.region, self.off + c0 * self.W * self.ds, self.off + c1 * self.W * self.ds)


def build_program(seqs):
    NT = sum(seqs) // T
    nc = bass.Bass("TRN2", target_bir_lowering=False)
    xT = nc.dram_tensor("xT", [NT, 128, 8, T], F32, kind="ExternalInput").ap()
    wall = nc.dram_tensor("wall", [L, 128, WTOT], F32, kind="ExternalInput").ap()
    par_d = nc.dram_tensor("par", [128, NPAR], F32, kind="ExternalInput").ap()
    cst_d = nc.dram_tensor("cst", [2, 128, SMAX], F32, kind="ExternalInput").ap()
    swp_d = nc.dram_tensor("swp", [128, 128], F32, kind="ExternalInput").ap()
    yT = nc.dram_tensor("yT", [NT, 128, 8, T], F32, kind="ExternalOutput").ap()
    wsc = nc.dram_tensor("wsc", [L, 128, WTOT], BF16, kind="Internal").ap()
    xs = nc.dram_tensor("xs", [NT, 128, 8, T], F32, kind="Internal").ap()
    zs = nc.dram_tensor("zs", [NT, 128, 8, T], BF16, kind="Internal").ap()

    with ExitStack() as st:
        SBBYTES = 212480
        sb_t = st.enter_context(nc.sbuf_tensor("sb", [128, SBBYTES // 4], F32))
        ps_t = st.enter_context(nc.psum_tensor("ps", [128, 8, 512], F32))
        P = Prog(nc, st)
        M = SB(sb_t)

        def PS(b, n=512):
            return ps_t[:, b, 0:n], ("ps", b * 2048, b * 2048 + n * 4)

        o_par = M.alloc(NPAR * 4)
        par = M.view(o_par, NPAR * 4, F32)
        o_swb = M.alloc(256)
        swb = M.view(o_swb, 256, BF16)
        o_one = M.alloc(256)
        ones = M.view(o_one, 256, BF16)
        o_eps = M.alloc(64)
        epsc = M.view(o_eps, 4, F32)
        KT = T3(M, M.alloc(NKV * SMAX * 2), NKV, SMAX, BF16)
        Vt_off = M.alloc((SMAX // 128) * 256 * 2)
        Vt = T3(M, Vt_off, SMAX // 128, 256, BF16)
        o_tile = M.off
        xtb = [T3(M, M.alloc(8 * T * 4), 8, T, F32), T3(M, M.alloc(8 * T * 4), 8, T, F32)]
        XT = {"cur": xtb[0]}
        sq = T3(M, M.alloc(8 * T * 2), 8, T, BF16)
        hTb = [T3(M, M.alloc(8 * T * 2), 8, T, BF16), T3(M, M.alloc(8 * T * 2), 8, T, BF16)]
        HT = {"cur": hTb[0]}

        class _CurH:
            def ch(self, *a, **k):
                return HT["cur"].ch(*a, **k)

            def iv(self, *a, **k):
                return HT["cur"].iv(*a, **k)
        hT = _CurH()
        QT_off = M.alloc(8 * T * 2)
        QT = T3(M, QT_off, 8, T, BF16, "qg", QT_off)
        gat_off = M.alloc(16 * T * 2)
        assert gat_off == QT_off + 8 * T * 2
        gat = T3(M, gat_off, 16, T, BF16, "qg", QT_off)
        hh = T3(M, QT_off, NFF, T, BF16, "qg", QT_off)
        o_big = M.alloc(8 * T * 4 + 8 * (T + 30) * 2)
        acc = T3(M, o_big, 8, T, F32, "big", o_big)
        zb = T3(M, o_big + 8 * T * 4, 8, T + 30, BF16, "big", o_big)
        mc = T3(M, M.alloc(8 * T * 4), 8, T, F32)
        mm = T3(M, QT_off, 8, T, BF16, "qg", QT_off)
        lnt = T3(M, M.alloc(2 * T * 4), 2, T, F32)
        PT = T3(M, M.alloc(4 * T * 2), 4, T, BF16)
        qgb = T3(M, M.alloc(2 * T * 2), 2, T, BF16)
        cs = T3(M, M.alloc(2 * T * 4), 2, T, F32)
        NMISC = 5
        rsn = T3(M, M.alloc(2 * T * 4), 2, T, F32)
        misc = T3(M, M.alloc(NMISC * T * 4), NMISC, T, F32)
        ring_off = M.alloc(NS * LOADE * 2)
        ring = T3(M, ring_off, NS, LOADE, BF16)
        assert M.off <= SBBYTES, M.off

        def pcol(c):
            return par[:, c:c + 1]

        P.dma("sp", par, par_d, writes=[("sb", o_par, o_par + NPAR * 4)])
        swf_t = T3(M, ring_off, 1, 128, F32, "swf", ring_off)
        swf = swf_t.ch(0)
        P.dma("sp", swf[0], swp_d, writes=[swf[1]])
        P.op("dve", lambda e: e.tensor_copy(out=swb, in_=swf[0]), reads=[swf[1]],
             writes=[("sb", o_swb, o_swb + 256)])
        P.op("pool", lambda e: e.memset(ones, 1.0), writes=[("sb", o_one, o_one + 256)])
        P.op("pool", lambda e: e.memset(epsc, EPS), writes=[("sb", o_eps, o_eps + 4)])

        P.barrier_all()

        assert seqs[0] <= SMAX // 2
        CVP = 1024
        stg = []
        for k_ in range(2):
            o_ = (SMAX // 128) * 256 * 2 // 2 + k_ * CVP * 4
            stg.append((M.view(Vt_off + o_, CVP * 4, F32), (Vt.region, o_, o_ + CVP * 4)))
        cvt = {"i": 0, "done": set()}

        sched = []
        for S in seqs:
            for l in range(L):
                for _ in range(S // T):
                    sched.append((l, "A"))
                for _ in range(S // T):
                    sched.append((l, "B"))
        loads = []
        for (l, ph) in sched:
            if ph == "A":
                for k in range(5):
                    loads.append((l, k * LOADE, LOADE))
            else:
                base = WA
                for k in range(21):
                    loads.append((l, base + k * LOADE, LOADE))
                base += 21 * LOADE
                for k in range(8):
                    loads.append((l, base + k * DFF, DFF))
        wst = {"cur": -1, "issued": 0}

        def w_acquire():
            wst["cur"] += 1
            cur = wst["cur"]
            while wst["issued"] < min(len(loads), cur + NS):
                i = wst["issued"]
                l, off, n = loads[i]
                s = i % NS
                ap, iv = ring.ch(s, 0, n)
                wreg = ("wsc%d" % l, off * 2, (off + n) * 2)
                if (l, off) in cvt["done"]:
                    P.dma("sp", ap, wsc[l, :, off:off + n], reads=[wreg], writes=[iv])
                else:
                    cvt["done"].add((l, off))
                    for a_ in range(0, n, CVP):
                        b_ = min(n, a_ + CVP)
                        sg = stg[cvt["i"] % 2]
                        sgap = sg[0][:, 0:b_ - a_]
                        sgiv = (sg[1][0], sg[1][1], sg[1][1] + (b_ - a_) * 4)
                        P.dma("sp", sgap, wall[l, :, off + a_:off + b_], writes=[sgiv])
                        dst = ring.ch(s, a_, b_)
                        if False:
                            P.op("pool", lambda e, dst=dst, sgap=sgap: e.tensor_copy(out=dst[0], in_=sgap),
                                 reads=[sgiv], writes=[dst[1]])
                        else:
                            P.op("act", lambda e, dst=dst, sgap=sgap: e.activation(out=dst[0], in_=sgap, func=AF.Copy),
                                 reads=[sgiv], writes=[dst[1]])
                        cvt["i"] += 1
                    P.dma("sp", wsc[l, :, off:off + n], ap, reads=[iv], writes=[wreg])
                wst["issued"] += 1
            return cur % NS

        def w_tile(s, u, kc=8, width=128, off=None):
            o = u * 1024 if off is None else off
            ap, iv = ring.ch(s, o, o + kc * width)
            return ap.rearrange("p (k m) -> p k m", k=kc), iv

        gen_banks = {"i": 0}

        def gbank():
            b = gen_banks["i"] % 8
            gen_banks["i"] += 1
            return b

        misc_i = {"i": 0}

        def mtile():
            i = misc_i["i"] % NMISC
            misc_i["i"] += 1
            return misc.ch(i)

        def mm_group(out, pairs, reads):
            n = len(pairs)
            fns = [(lambda e, a=a, b=b, i=i: e.matmul(out[0], lhsT=a, rhs=b, start=(i == 0), stop=(i == n - 1)))
                   for i, (a, b) in enumerate(pairs)]
            P.op("pe", fns, reads=reads, writes=[out[1]])

        def pe_warm(n):
            if n <= 0:
                return
            pb = PS(gbank(), 128)
            fns = [(lambda e: e.matmul(pb[0], lhsT=ones, rhs=ones, start=True, stop=True)) for _ in range(n)]
            P.op("pe", fns, reads=[], writes=[pb[1]])

        def rstd_from(ps_sum, nfeat):
            va = mtile()
            P.op("act", lambda e: e.activation(out=va[0], in_=ps_sum[0], func=AF.Ln, scale=1.0 / nfeat, bias=epsc),
                 reads=[ps_sum[1]], writes=[va[1]])
            rs = mtile()
            P.op("act", lambda e: e.activation(out=rs[0], in_=va[0], func=AF.Exp, scale=-0.5),
                 reads=[va[1]], writes=[rs[1]])
            return rs

        SQ_ENG = ("act", "pool", "act", "dve", "act", "pool", "act", "dve")

        def norm1a(xb):
            for c in range(8):
                a, o = xb.ch(c), sq.ch(c)
                if SQ_ENG[c] == "act":
                    P.op("act", lambda e, a=a, o=o: e.activation(out=o[0], in_=a[0], func=AF.Square),
                         reads=[a[1]], writes=[o[1]])
                else:
                    P.op(SQ_ENG[c], lambda e, a=a, o=o: e.tensor_tensor(out=o[0], in0=a[0], in1=a[0], op=ALU.mult),
                         reads=[a[1]], writes=[o[1]])

        def norm1b(rs):
            pb = PS(gbank())
            mm_group(pb, [(ones, sq.ch(c)[0]) for c in range(8)], reads=[sq.iv()])
            va = mtile()
            P.op("act", lambda e: e.activation(out=va[0], in_=pb[0], func=AF.Ln, scale=1.0 / D, bias=epsc),
                 reads=[pb[1]], writes=[va[1]])
            P.op("act", lambda e: e.activation(out=rs[0], in_=va[0], func=AF.Exp, scale=-0.5),
                 reads=[va[1]], writes=[rs[1]])

        def norm1(xb, rs):
            norm1a(xb)
            norm1b(rs)

        def norm2(xb, rs, gcol0, hbuf=None):
            hbuf = HT["cur"] if hbuf is None else hbuf
            for c in range(8):
                a, o = xb.ch(c), hbuf.ch(c)
                P.op("dve", lambda e, a=a, o=o, c=c: e.scalar_tensor_tensor(
                    out=o[0], in0=a[0], scalar=pcol(gcol0 + c), in1=rs[0], op0=ALU.mult, op1=ALU.mult),
                    reads=[a[1], rs[1]], writes=[o[1]])

        def rmsnorm_to_hT(gcol0):
            rs = mtile()
            norm1(XT["cur"], rs)
            norm2(XT["cur"], rs, gcol0)

        pend = {"rs": None, "h": False}

        def start_norm(gcol0):
            if pend["rs"] is None:
                rs = rsn.ch(0)
                norm1(XT["cur"], rs)
            else:
                rs = pend["rs"]
            if not pend["h"]:
                norm2(XT["cur"], rs, gcol0)
            pend["rs"], pend["h"] = None, False

        def qk_s1(pb, gcol, slot):
            qg = qgb.ch(slot)
            s2 = sq.ch(slot)
            P.op("act", lambda e: e.activation(out=qg[0], in_=pb[0], func=AF.Copy, scale=pcol(gcol)),
                 reads=[pb[1]], writes=[qg[1]])
            P.op("act", lambda e: e.activation(out=s2[0], in_=pb[0], func=AF.Square),
                 reads=[pb[1]], writes=[s2[1]])

        def qk_s2(out, slot):
            qg = qgb.ch(slot)
            s2 = sq.ch(slot)
            pss = PS(gbank())
            mm_group(pss, [(ones, s2[0])], reads=[s2[1]])
            psw = PS(gbank())
            mm_group(psw, [(swb, qg[0])], reads=[qg[1]])
            rs = rstd_from(pss, HD)
            t1, t2 = mtile(), mtile()
            cosv, sinv = cs.ch(0), cs.ch(1)
            P.op("pool", lambda e: e.tensor_tensor(out=t1[0], in0=qg[0], in1=cosv[0], op=ALU.mult),
                 reads=[qg[1], cosv[1]], writes=[t1[1]])
            P.op("dve", lambda e: e.tensor_tensor(out=t2[0], in0=psw[0], in1=sinv[0], op=ALU.mult),
                 reads=[psw[1], sinv[1]], writes=[t2[1]])
            P.op("pool", lambda e: e.tensor_tensor(out=t1[0], in0=t1[0], in1=t2[0], op=ALU.add),
                 reads=[t1[1], t2[1]], writes=[t1[1]])
            P.op("dve", lambda e: e.tensor_tensor(out=out[0], in0=t1[0], in1=rs[0], op=ALU.mult),
                 reads=[t1[1], rs[1]], writes=[out[1]])

        XQ = {"q": "act"}
        IDX = {"i": 0}
        DEFER = []

        def flush_deferred():
            while DEFER:
                DEFER.pop(0)()

        def load_x(l, g, xb):
            src = xT if l == 0 else xs
            rd = [] if l == 0 else [("xs", g, g + 1)]
            P.dma(XQ["q"], xb.ap, src[g], reads=rd, writes=[xb.iv()])

        def load_cs(tpos):
            for i in range(2):
                ap, iv = cs.ch(i)
                P.dma("act", ap, cst_d[i, :, tpos:tpos + T], writes=[iv])

        def phase_a(l, g, ti, nt, nxt, xnext):
            pc = l * PL
            xt = XT["cur"]
            load_cs(ti * T)
            start_norm(pc + C_GMIX)
            s = w_acquire()
            wv, wviv = w_tile(s, 2, kc=8, width=256)

            def vblk(blk):
                pb = PS(gbank(), 256)
                mm_group(pb, [(hT.ch(k, blk * 128, (blk + 1) * 128)[0], wv[:, k, :]) for k in range(8)],
                         reads=[wviv, hT.iv()])
                o = Vt.ch(ti * 4 + blk)
                P.op("act", lambda e, o=o, pb=pb: e.activation(out=o[0], in_=pb[0], func=AF.Copy),
                     reads=[pb[1]], writes=[o[1]])
            for j in range(NKV):
                w, wiv = w_tile(s, j)
                pb = PS(gbank())
                mm_group(pb, [(w[:, k, :], hT.ch(k)[0]) for k in range(8)], reads=[wiv, hT.iv()])
                qk_s1(pb, pc + C_GK, j)
            flush_deferred()
            if nxt is not None:
                load_x(nxt[0], nxt[3], xnext)
            g_b0 = g - ti
            pre = ti >= 2
            if pre and BG["for"] != (l, g_b0):
                load_zb(g_b0, 0, nt)
                build_bg(l, g_b0, hTb[(IDX["i"] + nt - ti) % 2])
            vblk(0)
            vblk(1)
            qk_s2(KT.ch(0, ti * T, (ti + 1) * T), 0)
            vblk(2)
            qk_s2(KT.ch(1, ti * T, (ti + 1) * T), 1)
            if nxt is not None:
                pend["rs"] = rsn.ch(1 - pend.get("k", 0))
                pend["k"] = 1 - pend.get("k", 0)
                norm1a(xnext)
            vblk(3)
            for c in range(8):
                if nxt is not None and c == 1:
                    norm1b(pend["rs"])
                if nxt is not None and c == 2:
                    norm2(xnext, pend["rs"], nxt[0] * PL + C_GMIX, HT["next"])
                    pend["h"] = True
                if c % 2 == 0:
                    s = w_acquire()
                wa, waiv = w_tile(s, (c % 2) * 2)
                wg, wgiv = w_tile(s, (c % 2) * 2 + 1)
                pa, pg = PS(gbank()), PS(gbank())
                mm_group(pa, [(wa[:, k, :], hT.ch(k)[0]) for k in range(8)], reads=[waiv, hT.iv()])
                mm_group(pg, [(wg[:, k, :], hT.ch(k)[0]) for k in range(8)], reads=[wgiv, hT.iv()])
                sg = mtile()
                P.op("act", lambda e, sg=sg, pg=pg: e.activation(out=sg[0], in_=pg[0], func=AF.Sigmoid),
                     reads=[pg[1]], writes=[sg[1]])
                o = QT.ch(c)
                P.op("dve", lambda e, o=o, pa=pa, sg=sg: e.tensor_tensor(out=o[0], in0=pa[0], in1=sg[0], op=ALU.mult),
                     reads=[pa[1], sg[1]], writes=[o[1]])
                if pre:
                    conv_step(CONV_A, True)
            DEFER.append(lambda: P.dma("act", zs[g], QT.ap, reads=[QT.iv()], writes=[("zs", g, g + 1)]))

        BG = {"q": [], "i": 0, "ntap": 0, "for": None}

        def load_zb(g, ti, nt):
            P.dma("sp", zb.ap[:, :, 15:15 + T], zs[g], reads=[("zs", g, g + 1)], writes=[zb.iv()])
            if ti > 0:
                P.dma("sp", zb.ap[:, :, 0:15], zs[g - 1, :, :, T - 15:T], reads=[("zs", g - 1, g)], writes=[zb.iv()])
            else:
                P.op("pool", lambda e: e.memset(zb.ap[:, :, 0:15], 0.0), writes=[zb.iv()])
            if ti < nt - 1:
                P.dma("sp", zb.ap[:, :, 15 + T:30 + T], zs[g + 1, :, :, 0:15], reads=[("zs", g + 1, g + 2)],
                      writes=[zb.iv()])
            else:
                P.op("pool", lambda e: e.memset(zb.ap[:, :, 15 + T:30 + T], 0.0), writes=[zb.iv()])

        def build_bg(l, g, hT):
            pc = l * PL
            bg = []

            def tap_dve(k, c):
                zi, a = zb.ch(c, k, k + T), acc.ch(c)
                wcol = pcol(pc + C_WDW + k * 8 + c)
                if k == 0:
                    return lambda: P.op("dve", lambda e: e.tensor_scalar(
                        out=a[0], in0=zi[0], scalar1=wcol, scalar2=pcol(pc + C_BDW + c),
                        op0=ALU.mult, op1=ALU.add), reads=[zi[1]], writes=[a[1]])
                return lambda: P.op("dve", lambda e: e.scalar_tensor_tensor(
                    out=a[0], in0=zi[0], scalar=wcol, in1=a[0], op0=ALU.mult, op1=ALU.add),
                    reads=[zi[1], a[1]], writes=[a[1]])

            for k in range(CK):
                for c in range(8):
                    bg.append(tap_dve(k, c))
            ntap = len(bg)
            mu, lrs = lnt.ch(0), lnt.ch(1)

            def ln_evac(c):
                a, o1, o2 = acc.ch(c), hT.ch(c), sq.ch(c)

                def f():
                    P.op("act", lambda e: e.activation(out=o1[0], in_=a[0], func=AF.Copy), reads=[a[1]], writes=[o1[1]])
                    P.op("act", lambda e: e.activation(out=o2[0], in_=a[0], func=AF.Square), reads=[a[1]], writes=[o2[1]])
                return f
            for c in range(8):
                bg.append(ln_evac(c))
            bg.extend([None] * 12)
            pm = PS(3)
            bg.append(lambda: mm_group(pm, [(ones, hT.ch(c)[0]) for c in range(8)], reads=[hT.iv()]))
            bg.extend([None] * 4)
            bg.append(lambda: P.op("dve", lambda e: e.tensor_scalar(out=mu[0], in0=pm[0], scalar1=1.0 / D, scalar2=None,
                                                                     op0=ALU.mult), reads=[pm[1]], writes=[mu[1]]))
            bg.append(lambda: mm_group(pm, [(ones, sq.ch(c)[0]) for c in range(8)], reads=[sq.iv()]))
            bg.extend([None] * 4)

            def ln_rstd():
                vv = mtile()
                P.op("dve", lambda e: e.tensor_tensor(out=vv[0], in0=mu[0], in1=mu[0], op=ALU.mult),
                     reads=[mu[1]], writes=[vv[1]])
                P.op("dve", lambda e: e.scalar_tensor_tensor(out=vv[0], in0=pm[0], scalar=1.0 / D, in1=vv[0],
                                                             op0=ALU.mult, op1=ALU.subtract),
                     reads=[pm[1], vv[1]], writes=[vv[1]])
                P.op("act", lambda e: e.activation(out=vv[0], in_=vv[0], func=AF.Ln, bias=epsc),
                     reads=[vv[1]], writes=[vv[1]])
                P.op("act", lambda e: e.activation(out=lrs[0], in_=vv[0], func=AF.Exp, scale=-0.5),
                     reads=[vv[1]], writes=[lrs[1]])
            bg.append(ln_rstd)

            def ln_sub(c):
                a = acc.ch(c)
                return lambda: P.op("dve", lambda e: e.tensor_tensor(out=a[0], in0=a[0], in1=mu[0], op=ALU.subtract),
                                    reads=[a[1], mu[1]], writes=[a[1]])

            def ln_mul(c):
                a = acc.ch(c)
                return lambda: P.op("dve", lambda e: e.tensor_tensor(out=a[0], in0=a[0], in1=lrs[0], op=ALU.mult),
                                    reads=[a[1], lrs[1]], writes=[a[1]])

            tts = {}

            def ln_aff(c):
                a = acc.ch(c)
                return lambda: P.op("dve", lambda e: e.tensor_scalar(
                    out=a[0], in0=a[0], scalar1=pcol(pc + C_LNG + c), scalar2=pcol(pc + C_LNB + c),
                    op0=ALU.mult, op1=ALU.add), reads=[a[1]], writes=[a[1]])

            def ln_tanh(c):
                a = acc.ch(c)

                def f():
                    tts[c] = mtile()
                    tt = tts[c]
                    P.op("act", lambda e: e.activation(out=tt[0], in_=a[0], func=AF.Tanh, scale=0.5),
                         reads=[a[1]], writes=[tt[1]])
                return f

            def ln_out(c):
                a, o = acc.ch(c), hT.ch(c)

                def f():
                    tt = tts[c]
                    P.op("dve", lambda e: e.scalar_tensor_tensor(out=o[0], in0=tt[0], scalar=1.0, in1=a[0],
                                                                 op0=ALU.add, op1=ALU.mult),
                         reads=[tt[1], a[1]], writes=[o[1]])
                return f
            for c in range(8):
                bg.append(ln_sub(c))
            for c in range(8):
                bg.append(ln_mul(c))
            for c in range(8):
                bg.append(ln_aff(c))
            for c in range(10):
                if c < 8:
                    bg.append(ln_tanh(c))
                if c >= 2:
                    bg.append(ln_out(c - 2))
            BG["q"], BG["i"], BG["ntap"], BG["for"] = bg, 0, ntap, (l, g)

        def conv_step(n, taps_only=False):
            lim = BG["ntap"] if taps_only else len(BG["q"])
            while n > 0 and BG["i"] < lim:
                f = BG["q"][BG["i"]]
                BG["i"] += 1
                n -= 1
                if f is not None:
                    f()

        def phase_b(l, g, ti, nt, nxt, xnext):
            pc = l * PL
            S = nt * T
            nkc = S // 128
            xt = XT["cur"]
            load_cs(ti * T)
            if ti == 0:
                flush_deferred()
            if BG["for"] != (l, g):
                flush_deferred()
                load_zb(g, ti, nt)
                build_bg(l, g, HT["cur"])
            start_norm(pc + C_GMIX)
            conv_ops = BG["q"]
            cst_ = BG
            def gate_grp(c, s):
                w, wiv = w_tile(s, c % 4)
                pb = PS(gbank())
                mm_group(pb, [(w[:, k, :], hT.ch(k)[0]) for k in range(8)], reads=[wiv, hT.iv()])
                o = gat.ch(c)
                P.op("act", lambda e, o=o, pb=pb, c=c: e.activation(out=o[0], in_=pb[0], func=AF.Sigmoid,
                                                                  bias=pcol(pc + C_BG + c)),
                     reads=[pb[1]], writes=[o[1]])

            for h in range(NH):
                if h % 4 == 0:
                    s = w_acquire()
                w, wiv = w_tile(s, h % 4)
                pb = PS(gbank())
                mm_group(pb, [(w[:, k, :], hT.ch(k)[0]) for k in range(8)], reads=[wiv, hT.iv()])
                qk_s1(pb, pc + C_GQ, h % 2)
                if h == 2:
                    flush_deferred()
                conv_step(CONV_Q, True)
                if h >= 1:
                    qk_s2(QT.ch(h - 1), (h - 1) % 2)
            qk_s2(QT.ch(NH - 1), (NH - 1) % 2)
            for c in range(16):
                if c % 4 == 0:
                    s = w_acquire()
                gate_grp(c, s)
                conv_step(CONV_PROJ, True)
            items = [(h, kc) for h in range(NH) for kc in range(nkc)]
            LA = 2
            scale = float(HD) ** -0.5

            def head_epilogue(h):
                par_ = h % 2
                pv, sm = PS(4 + 2 * par_), PS(5 + 2 * par_)
                rc, an = mtile(), mtile()
                P.op("act", lambda e: e.activation(out=rc[0], in_=sm[0], func=AF.Ln), reads=[sm[1]], writes=[rc[1]])
                P.op("act", lambda e: e.activation(out=rc[0], in_=rc[0], func=AF.Exp, scale=-1.0), reads=[rc[1]], writes=[rc[1]])
                P.op("dve", lambda e: e.tensor_tensor(out=an[0], in0=pv[0], in1=rc[0], op=ALU.mult),
                     reads=[pv[1], rc[1]], writes=[an[1]])
                ga, o = gat.ch(h), mc.ch(h)
                P.op("pool", lambda e: e.tensor_tensor(out=o[0], in0=an[0], in1=ga[0], op=ALU.mult),
                     reads=[an[1], ga[1]], writes=[o[1]])

            n_it = len(items)
            for i in range(n_it + LA):
                if i < n_it:
                    h, kc = items[i]
                    j = h // (NH // NKV)
                    pb = PS(i % 3)
                    kt = KT.ch(j, kc * 128, (kc + 1) * 128)
                    q = QT.ch(h)
                    mm_group(pb, [(kt[0], q[0])], reads=[kt[1], q[1]])
                    pt = PT.ch(i % 4)
                    P.op("act", lambda e, pt=pt, pb=pb: e.activation(out=pt[0], in_=pb[0], func=AF.Exp, scale=scale),
                         reads=[pb[1]], writes=[pt[1]])
                    if i == 4 and nxt is not None:
                        load_x(nxt[0], nxt[3], xnext)
                    rem = len(conv_ops) - cst_["i"]
                    if rem > 0:
                        left = max(1, int(n_it * CONV_ATT_FRAC) - i)
                        conv_step(-(-rem // left))
                if i >= LA:
                    ii = i - LA
                    h, kc = items[ii]
                    j = h // (NH // NKV)
                    par_ = h % 2
                    pv, sm = PS(4 + 2 * par_), PS(5 + 2 * par_)
                    pt = PT.ch(ii % 4)
                    va, viv = Vt.ch(kc, j * 128, (j + 1) * 128)
                    first, last = (kc == 0), (kc == nkc - 1)
                    fns = [lambda e, pv=pv, va=va, pt=pt, first=first, last=last:
                           e.matmul(pv[0], lhsT=va, rhs=pt[0], start=first, stop=last),
                           lambda e, sm=sm, pt=pt, first=first, last=last:
                           e.matmul(sm[0], lhsT=ones, rhs=pt[0], start=first, stop=last)]
                    P.op("pe", fns, reads=[viv, pt[1]], writes=[pv[1], sm[1]])
                    if last:
                        head_epilogue(h)
            conv_step(len(conv_ops))
            for c in range(8):
                if c % 4 == 0:
                    s = w_acquire()
                w, wiv = w_tile(s, c % 4)
                pb = PS(gbank())
                mm_group(pb, [(w[:, k, :], hT.ch(k)[0]) for k in range(8)], reads=[wiv, hT.iv()])
                gc, tm, ma, o = gat.ch(8 + c), mtile(), mc.ch(c), mm.ch(c)
                P.op("dve", lambda e, tm=tm, pb=pb, gc=gc: e.scalar_tensor_tensor(
                    out=tm[0], in0=pb[0], scalar=0.5, in1=gc[0], op0=ALU.mult, op1=ALU.mult),
                    reads=[pb[1], gc[1]], writes=[tm[1]])
                P.op("pool", lambda e, tm=tm, ma=ma, o=o: e.tensor_tensor(out=o[0], in0=tm[0], in1=ma[0], op=ALU.add),
                     reads=[tm[1], ma[1]], writes=[o[1]])
            for c in range(8):
                if c % 4 == 0:
                    s = w_acquire()
                w, wiv = w_tile(s, c % 4)
                pb = PS(gbank())
                mm_group(pb, [(w[:, k, :], mm.ch(k)[0]) for k in range(8)], reads=[wiv, mm.iv()])
                x_ = xt.ch(c)
                P.op("dve", lambda e, x_=x_, pb=pb: e.tensor_tensor(out=x_[0], in0=pb[0], in1=x_[0], op=ALU.add),
                     reads=[pb[1], x_[1]], writes=[x_[1]])
                o_ = sq.ch(c)
                P.op("act", lambda e, x_=x_, o_=o_: e.activation(out=o_[0], in_=x_[0], func=AF.Square),
                     reads=[x_[1]], writes=[o_[1]])
            pipe = nxt is not None and nxt[1] == "B" and nxt[0] == l
            if pipe:
                load_zb(nxt[3], nxt[2], nt)
                build_bg(l, nxt[3], HT["next"])
            rs_f = mtile()
            pe_warm(WARM_A)
            norm1b(rs_f)
            pe_warm(WARM_B)
            norm2(xt, rs_f, pc + C_GFFN)
            for jf in range(NFF):
                if jf % 2 == 0:
                    s = w_acquire()
                wg, wgiv = w_tile(s, (jf % 2) * 2)
                wu, wuiv = w_tile(s, (jf % 2) * 2 + 1)
                pg, pu = PS(gbank()), PS(gbank())
                mm_group(pg, [(wg[:, k, :], hT.ch(k)[0]) for k in range(8)], reads=[wgiv, hT.iv()])
                mm_group(pu, [(wu[:, k, :], hT.ch(k)[0]) for k in range(8)], reads=[wuiv, hT.iv()])
                sg = mtile()
                P.op("act", lambda e, sg=sg, pg=pg: e.activation(out=sg[0], in_=pg[0], func=AF.Silu),
                     reads=[pg[1]], writes=[sg[1]])
                o = hh.ch(jf)
                P.op("dve", lambda e, o=o, pu=pu, sg=sg: e.tensor_tensor(out=o[0], in0=pu[0], in1=sg[0], op=ALU.mult),
                     reads=[pu[1], sg[1]], writes=[o[1]])
                if pipe:
                    conv_step(CONV_FFN_GU, True)
                if jf == 6 and nxt is not None:
                    norm1a(xnext)
            if nxt is not None:
                pend["rs"] = rsn.ch(1 - pend.get("k", 0))
                pend["k"] = 1 - pend.get("k", 0)
                norm1b(pend["rs"])
                norm2(xnext, pend["rs"], nxt[0] * PL + C_GMIX, HT["next"])
                pend["h"] = True
            for c in range(8):
                s = w_acquire()
                w, wiv = w_tile(s, 0, kc=NFF, width=128, off=0)
                pb = PS(gbank())
                mm_group(pb, [(w[:, k, :], hh.ch(k)[0]) for k in range(NFF)], reads=[wiv, hh.iv()])
                x_ = xt.ch(c)
                P.op("dve", lambda e, x_=x_, pb=pb: e.tensor_tensor(out=x_[0], in0=pb[0], in1=x_[0], op=ALU.add),
                     reads=[pb[1], x_[1]], writes=[x_[1]])
                if pipe:
                    conv_step(CONV_FFN_DN, True)
            if l < L - 1:
                DEFER.append(lambda: P.dma("act", xs[g], xt.ap, reads=[xt.iv()], writes=[("xs", g, g + 1)]))
            else:
                for c in range(8):
                    a, o = xt.ch(c), sq.ch(c)
                    P.op("act", lambda e, a=a, o=o: e.activation(out=o[0], in_=a[0], func=AF.Square),
                         reads=[a[1]], writes=[o[1]])
                pb = PS(gbank())
                mm_group(pb, [(ones, sq.ch(c)[0]) for c in range(8)], reads=[sq.iv()])
                rs = rstd_from(pb, D)
                for c in range(8):
                    a, o = xt.ch(c), mc.ch(c)
                    P.op("dve", lambda e, a=a, o=o, c=c: e.scalar_tensor_tensor(
                        out=o[0], in0=a[0], scalar=pcol(C_GF + c), in1=rs[0], op0=ALU.mult, op1=ALU.mult),
                        reads=[a[1], rs[1]], writes=[o[1]])
                DEFER.append(lambda: P.dma("act", yT[g], mc.ap, reads=[mc.iv()], writes=[("yT", g, g + 1)]))

        entries = []
        g0 = 0
        for S in seqs:
            nt = S // T
            for l in range(L):
                for ti in range(nt):
                    entries.append((l, "A", ti, g0 + ti, nt))
                for ti in range(nt):
                    entries.append((l, "B", ti, g0 + ti, nt))
            g0 += nt
        load_x(entries[0][0], entries[0][3], xtb[0])
        for i, (l, ph, ti, g, nt) in enumerate(entries):
            XT["cur"] = xtb[i % 2]
            IDX["i"] = i
            HT["cur"], HT["next"] = hTb[i % 2], hTb[(i + 1) % 2]
            nxt = entries[i + 1] if i + 1 < len(entries) else None
            if ph == "A":
                phase_a(l, g, ti, nt, nxt, xtb[(i + 1) % 2])
            else:
                phase_b(l, g, ti, nt, nxt, xtb[(i + 1) % 2])
        flush_deferred()
        assert wst["cur"] == len(loads) - 1, (wst, len(loads))
        P.barrier_all()
        P.run_blocks()
        build_program.stats = (dict(P.nops), P.nwaits)
    return nc


def _tile_std(W):
    K, Mo = W.shape
    kc, mj = K // 128, Mo // 128
    return W.reshape(kc, 128, mj, 128).transpose(1, 2, 0, 3).reshape(128, mj, kc * 128)


def prep_weights(w_in, w_pw, w_out, w_gu, w_down):
    out = np.empty((L, 128, WTOT), np.float32)
    for l in range(L):
        wi = w_in[l]
        wq = _tile_std(wi[:, 0:1024])
        wk = _tile_std(wi[:, 1024:1280])
        wv = wi[:, 1280:1536].reshape(8, 128, 256).transpose(1, 0, 2).reshape(128, 2048)
        wa = _tile_std(wi[:, 1536:2560])
        wg = _tile_std(wi[:, 2560:3584])
        wgl = _tile_std(wi[:, 3584:5632])
        glu = np.stack([wa, wg], axis=2).reshape(128, 16 * 1024)
        gg = _tile_std(w_gu[l][:, :DFF])
        uu = _tile_std(w_gu[l][:, DFF:])
        gu = np.stack([gg, uu], axis=2).reshape(128, 44 * 1024)
        wd = w_down[l].reshape(NFF, 128, 8, 128).transpose(1, 2, 0, 3).reshape(128, 8 * DFF)
        parts = [wk.reshape(128, -1), wv, glu, wq.reshape(128, -1), wgl.reshape(128, -1),
                 _tile_std(w_pw[l]).reshape(128, -1), _tile_std(w_out[l]).reshape(128, -1), gu, wd]
        row = np.concatenate(parts, axis=1)
        assert row.shape[1] == WTOT, row.shape
        out[l] = row
    return out


def prep_params(g_mix, g_q, g_k, w_dw, b_dw, ln_g, ln_b, b_gate, g_ffn, g_final):
    par = np.zeros((128, NPAR), np.float32)

    def cols(v):
        return v.reshape(-1, 128).T
    for l in range(L):
        b = l * PL
        par[:, b + C_GMIX:b + C_GMIX + 8] = cols(g_mix[l])
        par[:, b + C_GFFN:b + C_GFFN + 8] = cols(g_ffn[l])
        par[:, b + C_LNG:b + C_LNG + 8] = cols(ln_g[l])
        par[:, b + C_LNB:b + C_LNB + 8] = cols(ln_b[l])
        par[:, b + C_BDW:b + C_BDW + 8] = cols(b_dw[l])
        par[:, b + C_BG:b + C_BG + 16] = cols(b_gate[l])
        par[:, b + C_WDW:b + C_WDW + CK * 8] = w_dw[l].reshape(CK, 8, 128).transpose(2, 0, 1).reshape(128, CK * 8)
        par[:, b + C_GQ] = g_q[l]
        par[:, b + C_GK] = g_k[l]
    par[:, C_GF:C_GF + 8] = cols(g_final)
    return par


def prep_consts():
    d = np.arange(128)
    axis = d // 64
    half = (d % 64) // 32
    i = d % 32
    inv = (ROPE_THETA ** (-(i.astype(np.float32)) / np.float32(32))).astype(np.float32)
    t = np.arange(SMAX)
    pos = np.where(axis[:, None] == 0, (t // GRID_W)[None, :], (t % GRID_W)[None, :]).astype(np.float32)
    ang = (pos * inv[:, None]).astype(np.float32)
    cosT = np.cos(ang).astype(np.float32)
    sinT = (np.sin(ang) * np.where(half == 0, -1.0, 1.0)[:, None]).astype(np.float32)
    swp = np.zeros((128, 128), np.float32)
    pi = np.where((d % 64) < 32, d + 32, d - 32)
    swp[pi, d] = 1.0
    return np.stack([cosT, sinT]).astype(np.float32), swp


def to_tiles(x2d):
    nt = x2d.shape[0] // T
    return np.ascontiguousarray(x2d.reshape(nt, T, 8, 128).transpose(0, 3, 2, 1))


def from_tiles(y):
    nt = y.shape[0]
    return np.ascontiguousarray(y.transpose(0, 3, 2, 1)).reshape(nt * T, D)


_CACHE = {}


def run_cores(seq_lists_x, weights, n_cores):
    seqs = tuple(int(a.shape[0]) for a in seq_lists_x[0])
    if seqs not in _CACHE:
        _CACHE[seqs] = build_program(list(seqs))
    nc = _CACHE[seqs]
    wall, par, cst, swp = weights
    in_maps = []
    for c in range(n_cores):
        xcat = np.concatenate(seq_lists_x[c], axis=0)
        in_maps.append({"xT": to_tiles(xcat), "wall": wall, "par": par, "cst": cst, "swp": swp})
    res = run_bass_kernel_spmd(nc, in_maps, core_ids=list(range(n_cores)))
    outs = []
    for c in range(n_cores):
        y = from_tiles(np.asarray(res.results[c]["yT"]))
        o, k = [], 0
        for S in seqs:
            o.append(y[k:k + S])
            k += S
        outs.append(o)
    return outs


def kernel(x_prompt, x_sample, g_mix, w_in, g_q, g_k, w_dw, b_dw, ln_g, ln_b,
           w_pw, b_gate, w_out, g_ffn, w_gu, w_down, g_final):
    f = lambda a: np.asarray(a, dtype=np.float32)
    x_prompt, x_sample = f(x_prompt), f(x_sample)
    wall = prep_weights(f(w_in), f(w_pw), f(w_out), f(w_gu), f(w_down))
    par = prep_params(f(g_mix), f(g_q), f(g_k), f(w_dw), f(b_dw), f(ln_g), f(ln_b), f(b_gate), f(g_ffn), f(g_final))
    cst, swp = prep_consts()
    n = 8
    seq_lists = [[x_prompt[2 * c], x_prompt[2 * c + 1], x_sample[c]] for c in range(n)]
    outs = run_cores(seq_lists, (wall, par, cst, swp), n)
    y_prompt = np.stack([outs[c][i] for c in range(n) for i in range(2)], axis=0)
    y_sample = np.stack([outs[c][2] for c in range(n)], axis=0)
    return (y_prompt.astype(np.float32), y_sample.astype(np.float32))
```
